# Optimizing a Trainium2 kernel written in Bass

```python
import jax, jax.numpy as jnp
from jax import lax
import numpy as np

D_MODEL = 1024
BATCH = 8
SEQ = 4096
DEPTH = 2

N_MEM = 256
N_BRANCH = 4
GROUP_W = D_MODEL // N_BRANCH
N_SUB = 4
HEAD_DIM = GROUP_W // N_SUB
CHUNK = 128
GRID_W = 64
WIN_H_MAX = 8
WIN_W = 16
N_IN_SLOTS = 11
IN_COLS = N_IN_SLOTS * GROUP_W
EPS = 1e-6

kernel_name = "hybrid_parallel_gmlp_natten_fnet_memattn"


def rmsnorm(x, g):
    xf = x.astype(jnp.float32)
    y = xf * lax.rsqrt(jnp.mean(xf * xf, axis=-1, keepdims=True) + EPS)
    return (y * g.astype(jnp.float32)).astype(x.dtype)


def layernorm(x, g, b):
    xf = x.astype(jnp.float32)
    mu = jnp.mean(xf, axis=-1, keepdims=True)
    var = jnp.mean(jnp.square(xf - mu), axis=-1, keepdims=True)
    y = (xf - mu) * lax.rsqrt(var + EPS)
    return (y * g.astype(jnp.float32) + b.astype(jnp.float32)).astype(x.dtype)


def split_heads(x):
    return x.reshape(x.shape[:-1] + (N_SUB, HEAD_DIM))


def chunked_spatial_gating(u, v, ln_g, ln_b, w_s, b_s):
    B, S, _ = u.shape
    n_chunks = S // CHUNK
    vn = layernorm(v, ln_g, ln_b).reshape(B, n_chunks, CHUNK, N_SUB, HEAD_DIM)
    s = jnp.einsum('hpq,bnqhd->bnphd', w_s, vn) + b_s.T[None, None, :, :, None]
    return u * s.reshape(B, S, GROUP_W)


def neighbourhood_attention(q, k, v, qn_g, kn_g, rpb):
    B, S, _ = q.shape
    rows = S // GRID_W
    kh = min(WIN_H_MAX, rows)
    q = rmsnorm(split_heads(q), qn_g) * (HEAD_DIM ** -0.5)
    k = rmsnorm(split_heads(k), kn_g)
    v = split_heads(v)
    q = q.reshape(B, rows, GRID_W, N_SUB, HEAD_DIM)
    k = k.reshape(B, rows, GRID_W, N_SUB, HEAD_DIM)
    v = v.reshape(B, rows, GRID_W, N_SUB, HEAD_DIM)
    r = jnp.arange(rows)
    row_start = jnp.clip(r - kh // 2, 0, rows - kh)
    row_idx = row_start[:, None] + jnp.arange(kh)
    kg = k[:, row_idx]
    vg = v[:, row_idx]
    scores = jnp.einsum('brchd,brkwhd->bhrckw', q, kg).astype(jnp.float32)
    c = jnp.arange(GRID_W)
    col_start = jnp.clip(c - WIN_W // 2, 0, GRID_W - WIN_W)
    col_ok = (c[None, :] >= col_start[:, None]) & (c[None, :] < col_start[:, None] + WIN_W)
    d_row = row_idx - r[:, None] + (WIN_H_MAX - 1)
    d_col = jnp.clip(c[None, :] - c[:, None] + (WIN_W - 1), 0, 2 * WIN_W - 2)
    bias = rpb[:, d_row[:, None, :, None], d_col[None, :, None, :]]
    scores = scores + bias[None].astype(jnp.float32)
    scores = jnp.where(col_ok[None, None, None, :, None, :], scores, -jnp.inf)
    shp = scores.shape
    p = jax.nn.softmax(scores.reshape(shp[:4] + (kh * GRID_W,)), axis=-1).reshape(shp).astype(v.dtype)
    out = jnp.einsum('bhrckw,brkwhd->brchd', p, vg)
    return out.reshape(B, S, GROUP_W)


def fourier_mixing(f, w_f):
    B, S, _ = f.shape
    fg = f.astype(jnp.float32).reshape(B, S, N_SUB, HEAD_DIM)
    z = jnp.fft.fft2(fg, axes=(1, 3), norm='ortho').real.astype(f.dtype)
    return jnp.einsum('bshd,hde->bshe', z, w_f).reshape(B, S, GROUP_W)


def memory_attention(q, mem_n, w_mkv, qn_g, kn_g):
    B, S, _ = q.shape
    km, vm = jnp.split(mem_n @ w_mkv, 2, axis=-1)
    q = rmsnorm(split_heads(q), qn_g) * (HEAD_DIM ** -0.5)
    km = rmsnorm(split_heads(km), kn_g)
    vm = split_heads(vm)
    s = jnp.einsum('bshd,bmhd->bhsm', q, km).astype(jnp.float32)
    p = jax.nn.softmax(s, axis=-1).astype(vm.dtype)
    return jnp.einsum('bhsm,bmhd->bshd', p, vm).reshape(B, S, GROUP_W)


def setup_inputs(seed: int = 0) -> dict:
    key = jax.random.key(seed)
    ks = jax.random.split(key, 20)
    nrm = lambda k, shp: jax.random.normal(k, shp, jnp.float32)
    return {
        "x": nrm(ks[0], (BATCH, SEQ, D_MODEL)),
        "mem": nrm(ks[1], (BATCH, N_MEM, D_MODEL)),
        "norm_g": 1.0 + 0.02 * nrm(ks[2], (DEPTH, D_MODEL)),
        "w_in": nrm(ks[3], (DEPTH, D_MODEL, IN_COLS)) * D_MODEL ** -0.5,
        "w_out": nrm(ks[4], (DEPTH, D_MODEL, D_MODEL)) * D_MODEL ** -0.5,
        "gm_ln_g": 1.0 + 0.02 * nrm(ks[5], (DEPTH, GROUP_W)),
        "gm_ln_b": 0.02 * nrm(ks[6], (DEPTH, GROUP_W)),
        "gm_w_s": nrm(ks[7], (DEPTH, N_SUB, CHUNK, CHUNK)) * 0.5 * CHUNK ** -0.5,
        "gm_b_s": 1.0 + 0.02 * nrm(ks[8], (DEPTH, N_SUB, CHUNK)),
        "na_qn_g": 1.0 + 0.02 * nrm(ks[9], (DEPTH, HEAD_DIM)),
        "na_kn_g": 1.0 + 0.02 * nrm(ks[10], (DEPTH, HEAD_DIM)),
        "na_rpb": 0.1 * nrm(ks[11], (DEPTH, N_SUB, 2 * WIN_H_MAX - 1, 2 * WIN_W - 1)),
        "fn_w": nrm(ks[12], (DEPTH, N_SUB, HEAD_DIM, HEAD_DIM)) * HEAD_DIM ** -0.5,
        "mem_norm_g": 1.0 + 0.02 * nrm(ks[13], (DEPTH, D_MODEL)),
        "mem_w_kv": nrm(ks[14], (DEPTH, D_MODEL, 2 * GROUP_W)) * D_MODEL ** -0.5,
        "mem_qn_g": 1.0 + 0.02 * nrm(ks[15], (DEPTH, HEAD_DIM)),
        "mem_kn_g": 1.0 + 0.02 * nrm(ks[16], (DEPTH, HEAD_DIM)),
    }


def reference(x, mem, norm_g, w_in, w_out, gm_ln_g, gm_ln_b, gm_w_s, gm_b_s,
              na_qn_g, na_kn_g, na_rpb, fn_w, mem_norm_g, mem_w_kv, mem_qn_g, mem_kn_g):
    for l in range(DEPTH):
        h = rmsnorm(x, norm_g[l])
        proj = h @ w_in[l]
        a_u, a_v, a_g, b_q, b_k, b_v, b_g, c_in, c_g, d_q, d_g = jnp.split(proj, N_IN_SLOTS, axis=-1)
        y_a = chunked_spatial_gating(a_u, a_v, gm_ln_g[l], gm_ln_b[l], gm_w_s[l], gm_b_s[l]) * jax.nn.silu(a_g)
        y_b = neighbourhood_attention(b_q, b_k, b_v, na_qn_g[l], na_kn_g[l], na_rpb[l]) * jax.nn.silu(b_g)
        y_c = fourier_mixing(c_in, fn_w[l]) * jax.nn.silu(c_g)
        mem_n = rmsnorm(mem, mem_norm_g[l])
        y_d = memory_attention(d_q, mem_n, mem_w_kv[l], mem_qn_g[l], mem_kn_g[l]) * jax.nn.silu(d_g)
        y = jnp.concatenate([y_a, y_b, y_c, y_d], axis=-1) @ w_out[l]
        x = x + y
    return x
```

```python
from contextlib import ExitStack
import numpy as np
import ml_dtypes
import concourse.bass as bass
import concourse.mybir as mybir
from concourse.bass_utils import run_bass_kernel_spmd

F32 = mybir.dt.float32
BF16 = mybir.dt.bfloat16
AF = mybir.ActivationFunctionType
ALU = mybir.AluOpType

D = 1024
SEQ = 4096
NT = 32
NG = 8
INC = 2816
EPS = 1e-6
NPAR = 788
REORDER = True
PE_FREE = (1, 2, 3)
C_AU, C_AV, C_AG, C_BQ, C_BK, C_BV, C_BG, C_CI, C_CG, C_DQ, C_DG = [i * 256 for i in range(11)]


class _FakeIns:
    def then_inc(self, *a, **k):
        return self


class _FakeEng:
    def __init__(self):
        self.calls = []

    def __getattr__(self, name):
        def f(*a, **k):
            self.calls.append((name, a, k))
            return _FakeIns()
        return f


def _free(ap):
    n = 1
    for d in ap.shape[1:]:
        n *= d
    return n


def _estimate_cost(eng, fn, dma):
    if fn is None:
        return 0.0
    fe = _FakeEng()
    try:
        fn(fe)
        name, a, k = fe.calls[0]
    except Exception:
        name, a, k = "?", (), {}
    try:
        if dma is not None:
            src = k.get("in_")
            nbytes = _free(src) * src.shape[0] * (2 if src.dtype == BF16 else 4)
            return 2.0 + nbytes / 120e3
        if eng == "pe":
            if name == "matmul":
                n = _free(k["rhs"])
            else:
                n = _free(k["in_"])
            return max(n, 96) / 1350.0 + 0.04
        out = k.get("out", a[0] if a else None)
        n = _free(out)
        if eng == "act":
            return 0.25 + n / 1200.0
        if name == "reciprocal":
            return 0.08 + n * 6.0 / 960.0
        return 0.08 + n / 960.0
    except Exception:
        return 4.0 if dma is not None else {"pe": 0.2, "act": 0.6, "dve": 0.5, "pool": 0.5, "sp": 0.05}[eng]


class _Ins:
    __slots__ = ("eng", "fn", "deps", "dma", "dma_val", "needs_sig", "sig", "idx", "cost", "pos", "waits", "wtok", "phase")


class Sched:
    ENGS = ("pe", "act", "dve", "pool", "sp")
    LAT = 1.2
    FIXED = ()

    def __init__(self, reorder=True):
        self.ins = []
        self.lastw = {}
        self.readers = {}
        self.dma_cnt = {}
        self.reorder = reorder
        self.phase = 0
        self.pe_free = set(PE_FREE)

    def op(self, eng, fn, r=(), w=(), dma=None, cost=None):
        i = len(self.ins)
        deps = set()
        for t in r:
            if t in self.lastw:
                deps.add(self.lastw[t])
        for t in w:
            if t in self.lastw:
                deps.add(self.lastw[t])
            deps.update(self.readers.get(t, ()))
        rec = _Ins()
        rec.eng, rec.fn, rec.dma, rec.idx = eng, fn, dma, i
        rec.needs_sig = False
        rec.sig = None
        rec.dma_val = None
        if cost is None:
            cost = _estimate_cost(eng, fn, dma)
        rec.cost = cost
        if dma is not None:
            self.dma_cnt[dma] = self.dma_cnt.get(dma, 0) + 16
            rec.dma_val = self.dma_cnt[dma]
        rec.deps = sorted(deps)
        rec.wtok = list(w)
        rec.phase = self.phase
        self.ins.append(rec)
        for t in r:
            self.readers.setdefault(t, []).append(i)
        for t in w:
            self.lastw[t] = i
            self.readers[t] = []
        return i

    def _schedule(self):
        import heapq
        ins = self.ins
        n = len(ins)
        order = {e: [] for e in self.ENGS}
        if not self.reorder:
            for rec in ins:
                order[rec.eng].append(rec.idx)
            return order
        gid = [0] * n
        members = []
        last_pe = None
        for rec in ins:
            key = None
            if rec.eng == "pe" and rec.dma is None and rec.fn is not None:
                key = tuple(rec.wtok)
            if key is not None and last_pe is not None and last_pe[0] == key:
                g = last_pe[1]
                members[g].append(rec.idx)
            else:
                g = len(members)
                members.append([rec.idx])
            if rec.eng == "pe":
                last_pe = (key, g)
            gid[rec.idx] = g
        ng = len(members)
        ndeps = [set() for _ in range(ng)]
        prev = {}
        for rec in ins:
            g = gid[rec.idx]
            for j in rec.deps:
                if gid[j] != g:
                    ndeps[g].add(gid[j])
            if rec.eng in self.FIXED or rec.eng == "pe":
                fixed = rec.eng in self.FIXED or rec.phase not in self.pe_free
                if fixed and rec.eng in prev and prev[rec.eng] != g:
                    ndeps[g].add(prev[rec.eng])
                prev[rec.eng] = g
        succ = [[] for _ in range(ng)]
        npred = [len(d) for d in ndeps]
        for g, d in enumerate(ndeps):
            for j in d:
                succ[j].append(g)
        geng = [ins[m[0]].eng for m in members]
        gcost = [sum(ins[i].cost for i in m) for m in members]
        blev = [0.0] * ng
        for g in range(ng - 1, -1, -1):
            m = 0.0
            for k in succ[g]:
                if blev[k] > m:
                    m = blev[k]
            blev[g] = gcost[g] + (m + self.LAT if succ[g] else 0.0)
        fin = [0.0] * ng
        ready_t = [0.0] * ng
        later = {e: [] for e in self.ENGS}
        now = {e: [] for e in self.ENGS}
        free = {e: 0.0 for e in self.ENGS}
        for g in range(ng):
            if npred[g] == 0:
                heapq.heappush(later[geng[g]], (0.0, g))
        done = 0
        while done < ng:
            best = None
            for e in self.ENGS:
                lt, nw = later[e], now[e]
                while lt and lt[0][0] <= free[e]:
                    k_ = heapq.heappop(lt)[1]
                    heapq.heappush(nw, (-blev[k_], k_))
                if nw:
                    cand = (free[e], nw[0][1], e, True)
                elif lt:
                    cand = (lt[0][0], lt[0][1], e, False)
                else:
                    continue
                if best is None or cand[:2] < best[:2]:
                    best = cand
            st, g, e, from_now = best
            if from_now:
                heapq.heappop(now[e])
            else:
                heapq.heappop(later[e])
            rec = ins[members[g][0]]
            order[e].extend(members[g])
            if rec.dma is not None:
                free[e] = st + 0.06
                fin[g] = st + gcost[g]
            elif rec.fn is None:
                free[e] = st
                fin[g] = st
            else:
                free[e] = st + gcost[g]
                fin[g] = st + gcost[g]
            done += 1
            for k in succ[g]:
                npred[k] -= 1
                t = fin[g] + (0.0 if (geng[k] == e == "pe" and rec.dma is None) else self.LAT)
                if t > ready_t[k]:
                    ready_t[k] = t
                if npred[k] == 0:
                    heapq.heappush(later[geng[k]], (ready_t[k], k))
        return order

    def plan(self):
        ins = self.ins
        order = self._schedule()
        for e in self.ENGS:
            for pos, i in enumerate(order[e]):
                ins[i].pos = pos
        for rec in ins:
            best = {}
            for j in rec.deps:
                d = ins[j]
                if d.dma is not None:
                    k = ("dma", d.dma)
                    v = d.dma_val
                else:
                    if d.eng == "pe" and rec.eng == "pe" and rec.dma is None:
                        assert d.pos < rec.pos
                        continue
                    k = d.eng
                    v = d.pos
                if k not in best or best[k][0] < v:
                    best[k] = (v, j)
            rec.waits = [j for (_, j) in best.values()]
            for j in rec.waits:
                if ins[j].dma is None:
                    ins[j].needs_sig = True
        cnt = {e: 0 for e in self.ENGS}
        for e in self.ENGS:
            for i in order[e]:
                rec = ins[i]
                if rec.needs_sig and rec.dma is None:
                    cnt[e] += 1
                    rec.sig = cnt[e]
        self._check(order)
        return order

    def _check(self, order):
        ins = self.ins
        ptr = {e: 0 for e in self.ENGS}
        executed = [False] * len(ins)
        total = sum(len(v) for v in order.values())
        n = 0
        progress = True
        while progress:
            progress = False
            for e in self.ENGS:
                while ptr[e] < len(order[e]):
                    rec = ins[order[e][ptr[e]]]
                    if all(executed[j] for j in rec.waits):
                        executed[rec.idx] = True
                        ptr[e] += 1
                        n += 1
                        progress = True
                    else:
                        break
        assert n == total, "schedule deadlocks (%d of %d)" % (n, total)

    def emit(self, nc, es):
        order = self.plan()
        engsem = {e: es.enter_context(nc.semaphore("s_" + e)) for e in self.ENGS}
        dmasem = {}
        for k in self.dma_cnt:
            dmasem[k] = es.enter_context(nc.semaphore("d_%d" % len(dmasem)))
        ins = self.ins

        def run(eng_name, eng):
            waited = {}
            for i in order[eng_name]:
                rec = ins[i]
                for j in rec.waits:
                    d = ins[j]
                    if d.dma is not None:
                        sem, val, sk = dmasem[d.dma], d.dma_val, ("dma", d.dma)
                    else:
                        sem, val, sk = engsem[d.eng], d.sig, d.eng
                    if waited.get(sk, 0) >= val:
                        continue
                    waited[sk] = val
                    eng.wait_ge(sem, val)
                if rec.fn is None:
                    continue
                bi = rec.fn(eng)
                if rec.dma is not None:
                    bi.then_inc(dmasem[rec.dma], 16)
                elif rec.needs_sig:
                    bi.then_inc(engsem[eng_name], 1)

        with nc.Block() as block:
            @block.tensor
            def _(e):
                run("pe", e)

            @block.scalar
            def _(e):
                run("act", e)

            @block.vector
            def _(e):
                run("dve", e)

            @block.gpsimd
            def _(e):
                run("pool", e)

            @block.sync
            def _(e):
                run("sp", e)


class Buf:
    def __init__(self, t, rl, tok, base=0, ar=None):
        self.t, self.rl, self.tok, self.base, self.ar = t, rl, tok, base, ar

    def ap(self, dims, off=0, p0=0, pn=128):
        return bass.AP(self.t, p0 * self.rl + self.base + off, [[self.rl, pn]] + [list(d) for d in dims])


def dap(t, off, dims):
    return bass.AP(t, off, [list(d) for d in dims])


def build(nlayers=2, debug=False):
    nc = bass.Bass("TRN2", target_bir_lowering=False)
    nc.allow_low_precision("bf16 matmul operands with fp32 PSUM accumulation")
    IN = lambda n, s, d=F32: nc.dram_tensor(n, s, d, kind="ExternalInput")
    x_d = IN("x", [SEQ, D])
    mem_d = IN("mem", [256, D])
    win_d = IN("w_in", [2, D, INC])
    wout_d = IN("w_out", [2, D, D])
    wkv_d = IN("w_kv", [2, D, 512])
    par_d = IN("par", [2, 128, NPAR])
    wst_d = IN("wst", [2, 128, 512])
    fnw_d = IN("fnw", [2, 64, 256])
    bias_d = IN("biasT", [2, 128, 4096])
    identb_d = IN("identb", [128, 128], BF16)
    bo_d = IN("bo", [128, 128], BF16)
    c64_d = IN("c64", [128, 128], BF16)
    r1_d = IN("r1", [128, 128], BF16)
    r3_d = IN("r3", [128, 8192], BF16)
    out_d = nc.dram_tensor("out", [SEQ, D], F32, kind="ExternalOutput")
    sk = "ExternalOutput" if debug else "Internal"
    hT_d = nc.dram_tensor("hT_s", [NG, 128, 4096], BF16, kind=sk)
    nrm_d = nc.dram_tensor("nrm_s", [6, 128, 4096], BF16, kind=sk)
    va_d = nc.dram_tensor("va_s", [128, NT * 260], BF16, kind=sk)
    cin_d = nc.dram_tensor("cin_s", [2, 128, 4096], BF16, kind=sk)
    zz_d = nc.dram_tensor("zz_s", [2, 128, 4096], BF16, kind=sk)

    S = Sched(reorder=REORDER)
    with ExitStack() as es:
        def sb(name, rl, dt, pn=128):
            return Buf(es.enter_context(nc.sbuf_tensor("sb_" + name, [pn, rl], dt)), rl, name)

        Wb = sb("Wb", 8 * INC, BF16)
        Wob = sb("Wob", 8 * D, BF16)
        par = sb("par", NPAR, F32)
        gg = sb("gg", 2, F32)
        wsb = sb("wsb", 512, BF16)
        ABh = sb("ABh", 512, BF16)
        EB = sb("EB", 4096, BF16)
        kmn = sb("kmn", 512, BF16)
        kmnz = sb("kmnz", 1024, BF16)
        Vm = sb("Vm", 2 * 260, BF16)
        identb = sb("identb", 128, BF16)
        bo = sb("bo", 128, BF16)
        c64 = sb("c64", 128, BF16)
        r1 = sb("r1", 128, BF16)
        wst = [sb("wstg%d" % i, 2048, F32) for i in range(2)]
        xs = [sb("xs%d" % i, D, F32) for i in range(3)]
        hs = [sb("hs%d" % i, D, BF16) for i in range(3)]
        st1 = [sb("st1_%d" % i, 8, F32) for i in range(2)]
        hTg = [sb("hTg%d" % i, 8 * 512, BF16) for i in range(2)]
        sqb = [sb("sqb%d" % i, 512, F32) for i in range(2)]
        sq16 = [sb("sq16_%d" % i, 512, BF16) for i in range(4)]
        rb = [sb("rb%d" % i, 512, F32) for i in range(4)]
        nst = [sb("nst%d" % i, 6 * 512, BF16) for i in range(2)]
        vst = [sb("vst%d" % i, 1040, BF16) for i in range(2)]
        cst = [sb("cst%d" % i, 2 * 512, BF16) for i in range(2)]
        arA = sb("arA", 8192, BF16)
        arB = sb("arB", 8192, BF16)
        Wkvb = arB
        r3b = [sb("r3b%d" % i, 1024, BF16) for i in range(2)]
        cinh = hTg[0]
        zs = hTg[1]
        PSB = []
        for i in range(8):
            PSB.append(Buf(es.enter_context(nc.psum_tensor("ps%d" % i, [128, 512], F32)), 512, "ps%d" % i))
        psb16 = [Buf(PSB[i].t.bitcast(BF16), 1024, PSB[i].tok) for i in range(8)]
        rr = {"ps": 0}

        def nextps():
            i = rr["ps"]
            rr["ps"] = (i + 1) % 5
            return i

        rr["ph"] = 0

        def nextph():
            i = rr["ph"]
            rr["ph"] = (i + 1) % 3
            return 5 + i

        def OP(eng, fn, rd=(), wr=(), dma=None, er=(), ew=()):
            r = list(er)
            w = list(ew)
            for b in rd:
                r.append(b.tok)
                if b.ar:
                    r.append(b.ar)
            for b in wr:
                w.append(b.tok)
                if b.ar:
                    r.append(b.ar)
            S.op(eng, fn, r=r, w=w, dma=dma)

        WbA = Buf(Wb.t, Wb.rl, "WbA")
        WbB = Buf(Wb.t, Wb.rl, "WbB")
        parS = Buf(par.t, par.rl, "parS")
        parL = Buf(par.t, par.rl, "parL")
        cnt = {}

        def rot(name, n):
            c = cnt.get(name, 0)
            cnt[name] = c + 1
            return c % n

        def dma_in(dst, dst_ap, src_ap, q="sp", extra_w=()):
            S.op(q, lambda e: e.dma_start(out=dst_ap, in_=src_ap), w=[dst.tok] + list(extra_w), dma="L_" + dst.tok)

        dma_in(identb, identb.ap([[1, 128]]), dap(identb_d, 0, [[128, 128], [1, 128]]))
        dma_in(bo, bo.ap([[1, 128]]), dap(bo_d, 0, [[128, 128], [1, 128]]))
        dma_in(c64, c64.ap([[1, 128]]), dap(c64_d, 0, [[128, 128], [1, 128]]))
        dma_in(r1, r1.ap([[1, 128]]), dap(r1_d, 0, [[128, 128], [1, 128]]))
        S.op("dve", lambda e: e.memset(Vm.ap([[1, 520]]), 1.0), w=[Vm.tok])

        def rstd_from_ssq(st, col, scale):
            a = st.ap([[1, 1]], off=col)
            S.op("act", lambda e: e.activation(out=a, in_=a, func=AF.Ln, scale=scale, bias=EPS), r=[st.tok], w=[st.tok])
            S.op("act", lambda e: e.activation(out=a, in_=a, func=AF.Exp, scale=-0.5), r=[st.tok], w=[st.tok])

        def norm_transpose(xbuf, dst, dst_off, dst_kstride):
            h = hs[rot("hs", 3)]
            st = st1[rot("st1", 2)]
            S.op("act", lambda e: e.activation(out=h.ap([[1, D]]), in_=xbuf.ap([[1, D]]), func=AF.Square, accum_out=st.ap([[1, 1]])),
                 r=[xbuf.tok], w=[h.tok, st.tok])
            rstd_from_ssq(st, 0, 1.0 / D)
            S.op("dve", lambda e: e.tensor_scalar(out=h.ap([[1, D]]), in0=xbuf.ap([[1, D]]), scalar1=st.ap([[1, 1]]), scalar2=None, op0=ALU.mult),
                 r=[xbuf.tok, st.tok], w=[h.tok])
            pi = nextps()
            pb = psb16[pi]
            for kc in range(8):
                S.op("pe", lambda e, kc=kc: e.transpose(out=pb.ap([[1, 128]], off=kc * 128), in_=h.ap([[1, 128]], off=kc * 128), identity=identb.ap([[1, 128]])),
                     r=[h.tok, identb.tok], w=[pb.tok])
            S.op("dve", lambda e: e.tensor_copy(out=dst.ap([[dst_kstride, 8], [1, 128]], off=dst_off), in_=pb.ap([[128, 8], [1, 128]])),
                 r=[pb.tok], w=[dst.tok])

        def proj_fm(hT, col, n=512, W=None, wrow=INC):
            W = W or WbA
            pi = nextps()
            pb = PSB[pi]
            for kc in range(8):
                S.op("pe", lambda e, kc=kc: e.matmul(pb.ap([[1, n]]), lhsT=W.ap([[1, 128]], off=kc * wrow + col), rhs=hT.ap([[1, n]], off=kc * n),
                                                      start=(kc == 0), stop=(kc == 7)),
                     r=[W.tok, hT.tok], w=[pb.tok])
            return pb

        def headnorm(pb, n, gcol, out_buf, out_ap):
            sq = sq16[rot("sq16", 4)]
            r_ = rb[rot("rb", 4)]
            S.op("act", lambda e: e.activation(out=sq.ap([[1, n]]), in_=pb.ap([[1, n]]), func=AF.Square), r=[pb.tok], w=[sq.tok])
            p2 = PSB[nextps()]
            S.op("pe", lambda e: e.matmul(p2.ap([[1, n]]), lhsT=bo.ap([[1, 128]]), rhs=sq.ap([[1, n]]), start=True, stop=True),
                 r=[bo.tok, sq.tok], w=[p2.tok])
            S.op("act", lambda e: e.activation(out=r_.ap([[1, n]]), in_=p2.ap([[1, n]]), func=AF.Ln, bias=EPS), r=[p2.tok], w=[r_.tok])
            S.op("act", lambda e: e.activation(out=r_.ap([[1, n]]), in_=r_.ap([[1, n]]), func=AF.Exp, scale=-0.5), r=[r_.tok], w=[r_.tok])
            sc = gcol if gcol is not None else 1.0
            rd = [pb.tok, r_.tok] + ([gg.tok] if gcol is not None else [])
            S.op("dve", lambda e: e.scalar_tensor_tensor(out=out_ap, in0=pb.ap([[1, n]]), scalar=sc, in1=r_.ap([[1, n]]), op0=ALU.mult, op1=ALU.mult),
                 r=rd, w=[out_buf.tok])

        for l in range(nlayers):
            xsrc = x_d if l == 0 else out_d
            def p1(g):
                hT = hTg[g % 2]
                for j in range(4):
                    t = g * 4 + j
                    xb = xs[rot("xs", 3)]
                    xsrc_ap = dap(xsrc, t * 128 * D, [[D, 128], [1, D]])
                    S.op("sp", lambda e, xb=xb, xsrc_ap=xsrc_ap: e.dma_start(out=xb.ap([[1, D]]), in_=xsrc_ap),
                         r=[("xo", t)] if l > 0 else [], w=[xb.tok], dma="L_" + xb.tok)
                    norm_transpose(xb, hT, j * 128, 512)
                S.op("pool", lambda e, hT=hT, g=g: e.dma_start(out=dap(hT_d, g * 128 * 4096, [[4096, 128], [1, 4096]]), in_=hT.ap([[1, 4096]])),
                     r=[hT.tok], w=[("hT_d", g)], dma="S_" + hT.tok)

            def p2(g):
                hT = hTg[g % 2]
                ns = nst[rot("nst", 2)]
                for si, (col, gcol) in enumerate(((C_BK, None), (C_BQ, gg.ap([[1, 1]], off=0)), (C_DQ, gg.ap([[1, 1]], off=1)))):
                    for p in range(2):
                        pb = proj_fm(hT, col + p * 128)
                        headnorm(pb, 512, gcol, ns, ns.ap([[1, 512]], off=(si * 2 + p) * 512))
                S.op("pool", lambda e, ns=ns, g=g: e.dma_start(out=dap(nrm_d, g * 512, [[4096, 128], [128 * 4096, 6], [1, 512]]), in_=ns.ap([[512, 6], [1, 512]])),
                     r=[ns.tok], w=[("nrm_d", g)], dma="S_" + ns.tok)
                vs = vst[rot("vst", 2)]
                for j in range(4):
                    pb = PSB[nextps()]
                    for kc in range(8):
                        S.op("pe", lambda e, kc=kc, j=j, pb=pb, hT=hT: e.matmul(pb.ap([[1, 256]]), lhsT=hT.ap([[1, 128]], off=kc * 512 + j * 128), rhs=Wb.ap([[1, 256]], off=kc * INC + C_BV),
                                                                           start=(kc == 0), stop=(kc == 7)),
                             r=[hT.tok, WbA.tok], w=[pb.tok])
                    S.op("dve", lambda e, j=j, pb=pb, vs=vs: e.tensor_copy(out=vs.ap([[65, 4], [1, 64]], off=j * 260), in_=pb.ap([[64, 4], [1, 64]])), r=[pb.tok], w=[vs.tok])
                S.op("pool", lambda e, vs=vs, g=g: e.dma_start(out=dap(va_d, g * 1040, [[NT * 260, 128], [1, 1040]]), in_=vs.ap([[1, 1040]])),
                     r=[vs.tok], w=[("va_d", g)], dma="S_" + vs.tok)
                cs = cst[rot("cst", 2)]
                for p in range(2):
                    pb = proj_fm(hT, C_CI + p * 128)
                    S.op("act", lambda e, p=p, pb=pb, cs=cs: e.activation(out=cs.ap([[1, 512]], off=p * 512), in_=pb.ap([[1, 512]]), func=AF.Copy), r=[pb.tok], w=[cs.tok])
                S.op("pool", lambda e, cs=cs, g=g: e.dma_start(out=dap(cin_d, g * 512, [[4096, 128], [128 * 4096, 2], [1, 512]]), in_=cs.ap([[512, 2], [1, 512]])),
                     r=[cs.tok], w=[("cin_d", g)], dma="S_" + cs.tok)

            S.phase = 1
            p1(0)
            p1(1)
            S.phase = 0
            for v in vst:
                S.op("dve", lambda e, v=v: e.memset(v.ap([[1, 1040]]), 1.0), w=[v.tok])
            dma_in(parS, parS.ap([[1, 20]]), dap(par_d, l * 128 * NPAR, [[NPAR, 128], [1, 20]]))
            for kc in range(8):
                w_ = wst[rot("wst", 2)]
                dma_in(w_, w_.ap([[1, 768]]), dap(win_d, (l * D + kc * 128) * INC + 768, [[INC, 128], [1, 768]]))
                S.op("act", lambda e, kc=kc, w_=w_: e.activation(out=Wb.ap([[1, 768]], off=kc * INC + 768), in_=w_.ap([[1, 768]]), func=AF.Copy, scale=parS.ap([[1, 1]], off=kc)),
                     r=[w_.tok, parS.tok], w=[WbA.tok])
                w_ = wst[rot("wst", 2)]
                dma_in(w_, w_.ap([[256, 2], [1, 256]]), dap(win_d, (l * D + kc * 128) * INC + 1792, [[INC, 128], [512, 2], [1, 256]]))
                S.op("act", lambda e, kc=kc, w_=w_: e.activation(out=Wb.ap([[512, 2], [1, 256]], off=kc * INC + 1792), in_=w_.ap([[256, 2], [1, 256]]), func=AF.Copy, scale=parS.ap([[1, 1]], off=kc)),
                     r=[w_.tok, parS.tok], w=[WbA.tok])
            for j, c0 in ((0, 16), (1, 18)):
                S.op("dve", lambda e, j=j, c0=c0: e.scalar_tensor_tensor(out=gg.ap([[1, 1]], off=j), in0=parS.ap([[1, 1]], off=c0), scalar=0.125, in1=parS.ap([[1, 1]], off=c0 + 1), op0=ALU.mult, op1=ALU.mult),
                     r=[parS.tok], w=[gg.tok])
            S.phase = 1
            for g in range(NG):
                p2(g)
                if g + 2 < NG:
                    p1(g + 2)

            S.phase = 0
            dma_in(parL, parL.ap([[1, NPAR - 20]], off=20), dap(par_d, l * 128 * NPAR + 20, [[NPAR, 128], [1, NPAR - 20]]))
            for kc in range(8):
                w_ = wst[rot("wst", 2)]
                dma_in(w_, w_.ap([[1, 768]]), dap(win_d, (l * D + kc * 128) * INC + 0, [[INC, 128], [1, 768]]))
                S.op("act", lambda e, kc=kc, w_=w_: e.activation(out=Wb.ap([[1, 768]], off=kc * INC + 0), in_=w_.ap([[1, 768]]), func=AF.Copy, scale=parS.ap([[1, 1]], off=kc)),
                     r=[w_.tok, parS.tok], w=[WbB.tok])
                w_ = wst[rot("wst", 2)]
                dma_in(w_, w_.ap([[256, 3], [1, 256]]), dap(win_d, (l * D + kc * 128) * INC + 1536, [[INC, 128], [512, 3], [1, 256]]))
                S.op("act", lambda e, kc=kc, w_=w_: e.activation(out=Wb.ap([[512, 3], [1, 256]], off=kc * INC + 1536), in_=w_.ap([[256, 3], [1, 256]]), func=AF.Copy, scale=parS.ap([[1, 1]], off=kc)),
                     r=[w_.tok, parS.tok], w=[WbB.tok])
            for kc in range(8):
                w_ = wst[rot("wst", 2)]
                dma_in(w_, w_.ap([[1, D]]), dap(wout_d, (l * D + kc * 128) * D, [[D, 128], [1, D]]))
                S.op("dve", lambda e, kc=kc, w_=w_: e.tensor_scalar(out=Wob.ap([[1, D]], off=kc * D), in0=w_.ap([[1, D]]), scalar1=0.5, scalar2=None, op0=ALU.mult),
                     r=[w_.tok], w=[Wob.tok])
            for kc in range(8):
                w_ = wst[rot("wst", 2)]
                dma_in(w_, w_.ap([[1, 512]]), dap(wkv_d, (l * D + kc * 128) * 512, [[512, 128], [1, 512]]))
                S.op("act", lambda e, kc=kc, w_=w_: e.activation(out=Wkvb.ap([[1, 512]], off=kc * 512), in_=w_.ap([[1, 512]]), func=AF.Copy, scale=parS.ap([[1, 1]], off=8 + kc)),
                     r=[w_.tok, parS.tok], w=[Wkvb.tok])
            w_ = wst[rot("wst", 2)]
            dma_in(w_, w_.ap([[1, 512]]), dap(wst_d, l * 128 * 512, [[512, 128], [1, 512]]))
            S.op("dve", lambda e, w_=w_: e.tensor_copy(out=wsb.ap([[1, 512]]), in_=w_.ap([[1, 512]])), r=[w_.tok], w=[wsb.tok])
            for hh in range(2):
                w_ = wst[rot("wst", 2)]
                dma_in(w_, w_.ap([[1, 2048]]), dap(bias_d, l * 128 * 4096 + hh * 2048, [[4096, 128], [1, 2048]]))
                S.op("act", lambda e, hh=hh, w_=w_: e.activation(out=EB.ap([[1, 2048]], off=hh * 2048), in_=w_.ap([[1, 2048]]), func=AF.Exp), r=[w_.tok], w=[EB.tok])
            w_ = wst[rot("wst", 2)]
            dma_in(w_, w_.ap([[1, 256]], pn=64), dap(fnw_d, l * 64 * 256, [[256, 64], [1, 256]]))
            fb = sq16[rot("sq16", 4)]
            S.op("dve", lambda e, fb=fb: e.memset(fb.ap([[1, 256]]), 0.0), w=[fb.tok])
            S.op("dve", lambda e, w_=w_, fb=fb: e.tensor_copy(out=fb.ap([[1, 256]], pn=64), in_=w_.ap([[1, 256]], pn=64)), r=[w_.tok, fb.tok], w=[fb.tok])
            pb = PSB[nextps()]
            for h in range(4):
                for wv in range(2):
                    S.op("pe", lambda e, h=h, wv=wv, fb=fb, pb=pb: e.matmul(pb.ap([[1, 64]], off=h * 128 + wv * 64, p0=(h % 2) * 64, pn=64), lhsT=c64.ap([[1, 64]], off=wv * 64),
                                                                         rhs=fb.ap([[1, 64]], off=h * 64), start=True, stop=True, tile_position=(0, (h % 2) * 64)),
                         r=[c64.tok, fb.tok], w=[pb.tok])
            S.op("dve", lambda e: e.memset(ABh.ap([[1, 512]]), 0.0), w=[ABh.tok])
            for hl in range(2):
                S.op("dve", lambda e, pb=pb, hl=hl: e.tensor_copy(out=ABh.ap([[256, 2], [1, 128]], off=hl * 128, p0=hl * 64, pn=64), in_=pb.ap([[256, 2], [1, 128]], off=hl * 128, p0=hl * 64, pn=64)),
                     r=[pb.tok, ABh.tok], w=[ABh.tok])
            memT = arA
            for mt in range(2):
                xb = xs[rot("xs", 3)]
                dma_in(xb, xb.ap([[1, D]]), dap(mem_d, mt * 128 * D, [[D, 128], [1, D]]))
                norm_transpose(xb, memT, mt * 128, 256)
            for p in range(2):
                pb = proj_fm(memT, p * 128, n=256, W=Wkvb, wrow=512)
                headnorm(pb, 256, None, kmn, kmn.ap([[1, 256]], off=p * 256))
            S.op("dve", lambda e: e.memset(kmnz.ap([[1, 1024]]), 0.0), w=[kmnz.tok])
            for hl in range(2):
                S.op("dve", lambda e, hl=hl: e.tensor_copy(out=kmnz.ap([[512, 2], [1, 256]], off=hl * 256, p0=hl * 64, pn=64), in_=kmn.ap([[256, 2], [1, 256]], p0=hl * 64, pn=64)),
                     r=[kmn.tok, kmnz.tok], w=[kmnz.tok])
            for mt in range(2):
                pb = PSB[nextps()]
                for kc in range(8):
                    S.op("pe", lambda e, kc=kc, mt=mt, pb=pb: e.matmul(pb.ap([[1, 256]]), lhsT=memT.ap([[1, 128]], off=kc * 256 + mt * 128), rhs=Wkvb.ap([[1, 256]], off=kc * 512 + 256),
                                                                  start=(kc == 0), stop=(kc == 7)),
                         r=[memT.tok, Wkvb.tok], w=[pb.tok])
                S.op("dve", lambda e, mt=mt, pb=pb: e.tensor_copy(out=Vm.ap([[65, 4], [1, 64]], off=mt * 260), in_=pb.ap([[64, 4], [1, 64]])), r=[pb.tok], w=[Vm.tok])

            S.phase = 2
            GG, PP = arA, arB
            for h in range(4):
                p, hl = h // 2, h % 2
                if hl == 0:
                    S.op("sp", lambda e, p=p: e.dma_start(out=cinh.ap([[1, 4096]]), in_=dap(cin_d, p * 128 * 4096, [[4096, 128], [1, 4096]])),
                         r=[("cin_d", g_) for g_ in range(NG)], w=[cinh.tok], dma="L_" + cinh.tok)
                    S.op("dve", lambda e: e.memset(GG.ap([[1, 8192]], p0=64, pn=64), 0.0), w=[GG.tok])
                for sb_ in range(16):
                    pb = PSB[nextps()]
                    for i in range(4):
                        s2 = sb_ * 4 + i
                        S.op("pe", lambda e, i=i, s2=s2, pb=pb, h=h: e.matmul(pb.ap([[1, 128]], off=i * 128, pn=64), lhsT=cinh.ap([[64, 64]], off=s2),
                                                                        rhs=ABh.ap([[1, 128]], off=h * 128), start=True, stop=True),
                             r=[cinh.tok, ABh.tok], w=[pb.tok])
                    S.op("act" if sb_ % 2 else "dve",
                         (lambda e, pb=pb, sb_=sb_: e.activation(out=GG.ap([[64, 4], [4096, 2], [1, 64]], off=sb_ * 256, pn=64), in_=pb.ap([[128, 4], [64, 2], [1, 64]], pn=64), func=AF.Copy)) if sb_ % 2 else
                         (lambda e, pb=pb, sb_=sb_: e.tensor_copy(out=GG.ap([[64, 4], [4096, 2], [1, 64]], off=sb_ * 256, pn=64), in_=pb.ap([[128, 4], [64, 2], [1, 64]], pn=64))),
                         r=[pb.tok], w=[GG.tok])
                for eb in range(16):
                    pb = PSB[nextps()]
                    for i in range(4):
                        e_ = eb * 4 + i
                        S.op("pe", lambda e, i=i, e_=e_, pb=pb: e.matmul(pb.ap([[1, 128]], off=i * 128), lhsT=GG.ap([[64, 128]], off=e_), rhs=r1.ap([[1, 128]]), start=True, stop=True),
                             r=[GG.tok, r1.tok], w=[pb.tok])
                    S.op("act" if eb % 2 else "dve",
                         (lambda e, pb=pb, eb=eb: e.activation(out=PP.ap([[1, 512]], off=eb * 512), in_=pb.ap([[1, 512]]), func=AF.Copy)) if eb % 2 else
                         (lambda e, pb=pb, eb=eb: e.tensor_copy(out=PP.ap([[1, 512]], off=eb * 512), in_=pb.ap([[1, 512]]))),
                         r=[pb.tok], w=[PP.tok])
                for kb in range(8):
                    rbuf = r3b[rot("r3b", 2)]
                    dma_in(rbuf, rbuf.ap([[1, 1024]]), dap(r3_d, kb * 1024, [[8192, 128], [1, 1024]]))
                    pb = PSB[nextps()]
                    for i in range(8):
                        k1 = kb * 8 + i
                        for c in range(2):
                            S.op("pe", lambda e, i=i, k1=k1, c=c, pb=pb, rbuf=rbuf, hl=hl: e.matmul(pb.ap([[1, 64]], off=i * 64, p0=hl * 64, pn=64), lhsT=PP.ap([[128, 64]], off=c * 64 + k1),
                                                                                             rhs=rbuf.ap([[1, 64]], off=(i * 2 + c) * 64), start=(c == 0), stop=(c == 1), tile_position=(0, hl * 64)),
                                 r=[PP.tok, rbuf.tok], w=[pb.tok])
                    S.op("dve", lambda e, pb=pb, kb=kb, hl=hl: e.tensor_copy(out=zs.ap([[1, 8], [64, 64]], off=kb * 8, p0=hl * 64, pn=64), in_=pb.ap([[64, 8], [1, 64]], p0=hl * 64, pn=64)),
                         r=[pb.tok], w=[zs.tok])
                if hl == 1:
                    S.op("pool", lambda e, p=p: e.dma_start(out=dap(zz_d, p * 128 * 4096, [[4096, 128], [1, 4096]]), in_=zs.ap([[1, 4096]])),
                         r=[zs.tok], w=[("zz_d", p)], dma="S_" + zs.tok)

            S.phase = 3
            S.op("dve", lambda e: e.memset(arA.ap([[1, 1]]), 0.0), w=[arA.tok])
            S.op("dve", lambda e: e.memset(arB.ap([[1, 1]]), 0.0), w=[arB.tok])
            yT = Buf(arA.t, 8192, "yT", 0, arA.tok)
            vw = Buf(arA.t, 8192, "vw", 4096, arA.tok)
            exb = [Buf(arA.t, 8192, "exb%d" % i, 4096 + 2080 + i * 320, arA.tok) for i in range(3)]
            ptb = [Buf(arA.t, 8192, "ptb%d" % i, 4096 + 2080 + 960 + i * 320, arA.tok) for i in range(3)]
            kw = Buf(arB.t, 8192, "kw", 0, arB.tok)
            gt = Buf(arB.t, 8192, "gt", 2048, arB.tok)
            ptm = [Buf(arB.t, 8192, "ptm%d" % i, 6144 + i * 1024, arB.tok) for i in range(2)]
            ug = [Buf(hs[i].t.bitcast(F32), 512, hs[i].tok) for i in range(2)]
            ft = sqb
            qz = [Buf(rb[i].t.bitcast(BF16), 1024, rb[i].tok) for i in range(2)]
            for i in range(2):
                S.op("dve", lambda e, i=i: e.memset(rb[i].ap([[1, 512]]), 0.0), w=[rb[i].tok])
            obb = vst[0]
            odb = vst[1]
            rec = st1[0]
            mv = st1[1]

            def win(g):
                tlo = min(max(8 * g - 4, 0), 56) // 2
                thi = (min(max(8 * g + 3, 0), 56) + 7) // 2
                return tlo, thi - tlo + 1

            def loads_a(g):
                hT = hTg[g % 2]
                ns = nst[g % 2]
                OP("sp", lambda e: e.dma_start(out=hT.ap([[1, 4096]]), in_=dap(hT_d, g * 128 * 4096, [[4096, 128], [1, 4096]])),
                   wr=[hT], er=[("hT_d", g)], dma="L_" + hT.tok)
                OP("sp", lambda e: e.dma_start(out=ns.ap([[512, 2], [1, 512]], off=2048), in_=dap(nrm_d, 4 * 128 * 4096 + g * 512, [[4096, 128], [128 * 4096, 2], [1, 512]])),
                   wr=[ns], er=[("nrm_d", g)], dma="L_" + ns.tok)
                OP("sp", lambda e: e.dma_start(out=ns.ap([[512, 2], [1, 512]]), in_=dap(zz_d, g * 512, [[4096, 128], [128 * 4096, 2], [1, 512]])),
                   wr=[ns], er=[("zz_d", 0), ("zz_d", 1)], dma="L_" + ns.tok)

            def loads_b(g):
                tlo, nt_ = win(g)
                OP("sp", lambda e: e.dma_start(out=kw.ap([[1024, 2], [1, nt_ * 128]]), in_=dap(nrm_d, tlo * 128, [[4096, 128], [128 * 4096, 2], [1, nt_ * 128]])),
                   wr=[kw], er=[("nrm_d", g_) for g_ in range(NG)], dma="L_kw")
                OP("sp", lambda e: e.dma_start(out=vw.ap([[1, nt_ * 260]]), in_=dap(va_d, tlo * 260, [[NT * 260, 128], [1, nt_ * 260]])),
                   wr=[vw], er=[("va_d", g_) for g_ in range(NG)], dma="L_vw")
                for p in range(2):
                    for hl in range(2):
                        OP("sp", lambda e, p=p, hl=hl: e.dma_start(out=qz[p].ap([[1, 512]], off=hl * 512, p0=hl * 64, pn=64), in_=dap(nrm_d, ((2 + p) * 128 + hl * 64) * 4096 + g * 512, [[4096, 64], [1, 512]])),
                           wr=[qz[p]], er=[("nrm_d", g)], dma="L_" + qz[p].tok)

            def proj3(hT, col):
                pb = PSB[nextps()]
                for kc in range(8):
                    OP("pe", lambda e, kc=kc: e.matmul(pb.ap([[1, 512]]), lhsT=Wb.ap([[1, 128]], off=kc * INC + col), rhs=hT.ap([[1, 512]], off=kc * 512),
                                                        start=(kc == 0), stop=(kc == 7)), rd=[WbB, hT], wr=[pb])
                return pb

            def do_group(g):
                hT = hTg[g % 2]
                ns = nst[g % 2]
                tlo, nt_ = win(g)
                if g + 1 < NG:
                    loads_a(g + 1)
                xbs = []
                for j in range(4):
                    t = g * 4 + j
                    xb = xs[rot("xs", 3)]
                    xbs.append(xb)
                    xsrc_ap = dap(xsrc, t * 128 * D, [[D, 128], [1, D]])
                    if j < 3:
                        OP("sp", lambda e, xb=xb, xsrc_ap=xsrc_ap: e.dma_start(out=xb.ap([[1, D]]), in_=xsrc_ap), wr=[xb], er=[("xo", t)] if l > 0 else [], dma="L_" + xb.tok)
                chunks = []

                def gate_chunk(si, col, p):
                    def f():
                        pb = proj3(hT, col + p * 128)
                        th = ft[rot("ft", 2)]
                        OP("act", lambda e: e.activation(out=th.ap([[1, 512]]), in_=pb.ap([[1, 512]]), func=AF.Tanh, scale=0.5), rd=[pb], wr=[th])
                        OP("dve", lambda e: e.scalar_tensor_tensor(out=gt.ap([[1, 512]], off=(si * 2 + p) * 512), in0=th.ap([[1, 512]]), scalar=1.0, in1=pb.ap([[1, 512]]), op0=ALU.add, op1=ALU.mult),
                           rd=[pb, th], wr=[gt])
                    return f

                for si, col in enumerate((C_AG, C_BG, C_CG, C_DG)):
                    for p in range(2):
                        chunks.append(gate_chunk(si, col, p))

                def c_chunk():
                    for p in range(2):
                        OP("dve", lambda e, p=p: e.tensor_tensor(out=yT.ap([[1, 512]], off=(4 + p) * 512), in0=ns.ap([[1, 512]], off=p * 512), in1=gt.ap([[1, 512]], off=(4 + p) * 512), op=ALU.mult),
                           rd=[ns, gt], wr=[yT])
                chunks.append(c_chunk)

                def u_chunk(p):
                    def f():
                        pb = proj3(hT, C_AU + p * 128)
                        OP("dve", lambda e: e.tensor_tensor(out=ug[p].ap([[1, 512]]), in0=pb.ap([[1, 512]]), in1=gt.ap([[1, 512]], off=p * 512), op=ALU.mult), rd=[pb, gt], wr=[ug[p]])
                    return f
                chunks.append(u_chunk(0))
                chunks.append(u_chunk(1))
                spbh = {}

                def v_chunk(j):
                    def f():
                        if j == 0:
                            spbh[0], spbh[1] = PSB[5], PSB[6]
                        spb = spbh
                        pb = PSB[nextps()]
                        for kc in range(8):
                            OP("pe", lambda e, kc=kc: e.matmul(pb.ap([[1, 256]]), lhsT=hT.ap([[1, 128]], off=kc * 512 + j * 128), rhs=Wb.ap([[1, 256]], off=kc * INC + C_AV),
                                                               start=(kc == 0), stop=(kc == 7)), rd=[hT, WbB], wr=[pb])
                        bst = ft[rot("ft", 2)]
                        OP("dve", lambda e: e.bn_stats(out=bst.ap([[1, 6]]), in_=pb.ap([[1, 256]])), rd=[pb], wr=[bst])
                        OP("dve", lambda e: e.bn_aggr(out=mv.ap([[1, 2]]), in_=bst.ap([[1, 6]])), rd=[bst], wr=[mv])
                        a = mv.ap([[1, 1]], off=1)
                        OP("act", lambda e: e.activation(out=a, in_=a, func=AF.Sqrt, bias=EPS), rd=[mv], wr=[mv])
                        OP("dve", lambda e: e.reciprocal(out=a, in_=a), rd=[mv], wr=[mv])
                        tmp = ft[rot("ft", 2)]
                        OP("dve", lambda e: e.tensor_scalar(out=tmp.ap([[1, 256]]), in0=pb.ap([[1, 256]]), scalar1=mv.ap([[1, 1]]), scalar2=mv.ap([[1, 1]], off=1), op0=ALU.subtract, op1=ALU.mult),
                           rd=[pb, mv], wr=[tmp])
                        OP("dve", lambda e: e.tensor_tensor(out=tmp.ap([[1, 256]]), in0=tmp.ap([[1, 256]]), in1=par.ap([[1, 256]], off=20), op=ALU.mult), rd=[tmp, parL], wr=[tmp])
                        vn = cst[rot("cst", 2)]
                        OP("dve", lambda e: e.tensor_tensor(out=vn.ap([[1, 256]]), in0=tmp.ap([[1, 256]]), in1=par.ap([[1, 256]], off=276), op=ALU.add), rd=[tmp, parL], wr=[vn])
                        for h in range(4):
                            p, hl = h // 2, h % 2
                            OP("pe", lambda e, h=h, p=p, hl=hl: e.matmul(spb[p].ap([[1, 128]], off=j * 128, p0=hl * 64, pn=64), lhsT=vn.ap([[1, 64]], off=h * 64), rhs=wsb.ap([[1, 128]], off=h * 128),
                                                                     start=True, stop=True, tile_position=(0, hl * 64)), rd=[vn, wsb], wr=[spb[p]])
                        if j == 3:
                            for p in range(2):
                                t1 = ft[rot("ft", 2)]
                                OP("dve", lambda e, p=p, t1=t1: e.tensor_tensor(out=t1.ap([[128, 4], [1, 128]]), in0=spb[p].ap([[128, 4], [1, 128]]), in1=par.ap([[0, 4], [1, 128]], off=532 + p * 128), op=ALU.add),
                                   rd=[spb[p], parL], wr=[t1])
                                OP("dve", lambda e, p=p, t1=t1: e.tensor_tensor(out=yT.ap([[1, 512]], off=p * 512), in0=t1.ap([[1, 512]]), in1=ug[p].ap([[1, 512]]), op=ALU.mult), rd=[t1, ug[p]], wr=[yT])
                    return f
                for j in range(4):
                    chunks.append(v_chunk(j))

                def d_chunk(h):
                    def f():
                        p, hl = h // 2, h % 2
                        pm = ptm[h % 2]
                        for mt in range(2):
                            pb = PSB[nextps()]
                            OP("pe", lambda e, pb=pb, mt=mt: e.matmul(pb.ap([[1, 512]]), lhsT=kmnz.ap([[1, 128]], off=(p * 2 + hl) * 256 + mt * 128),
                                                                      rhs=ns.ap([[1, 512]], off=(4 + p) * 512), start=True, stop=True), rd=[kmnz, ns], wr=[pb])
                            OP("act", lambda e, pb=pb, mt=mt: e.activation(out=pm.ap([[1, 512]], off=mt * 512), in_=pb.ap([[1, 512]]), func=AF.Exp), rd=[pb], wr=[pm])
                        ob_ = PSB[5 + h % 2]
                        for j in range(4):
                            for mt in range(2):
                                OP("pe", lambda e, j=j, mt=mt: e.matmul(ob_.ap([[1, 65]], off=j * 65), lhsT=pm.ap([[1, 128]], off=mt * 512 + j * 128), rhs=Vm.ap([[1, 65]], off=mt * 260 + h * 65),
                                                                        start=(mt == 0), stop=(mt == 1)), rd=[pm, Vm], wr=[ob_])
                        rc = Buf(rec.t, rec.rl, "recD", base=4)
                        OP("dve", lambda e: e.reciprocal(out=rc.ap([[1, 4]]), in_=ob_.ap([[65, 4]], off=64)), rd=[ob_], wr=[rc], er=[rec.tok])
                        OP("dve", lambda e: e.tensor_tensor(out=odb.ap([[256, 4], [1, 64]], off=h * 64), in0=ob_.ap([[65, 4], [1, 64]]), in1=rc.ap([[1, 4], [0, 64]]), op=ALU.mult),
                           rd=[ob_, rc], wr=[odb], er=[rec.tok])
                    return f
                for h in range(4):
                    chunks.append(d_chunk(h))

                def dT_chunk(p):
                    def f():
                        tb = psb16[5 + p]
                        for j in range(4):
                            OP("pe", lambda e, j=j: e.transpose(out=tb.ap([[1, 128]], off=j * 128), in_=odb.ap([[1, 128]], off=j * 256 + p * 128), identity=identb.ap([[1, 128]])),
                               rd=[odb, identb], wr=[tb])
                        OP("dve", lambda e: e.tensor_tensor(out=yT.ap([[1, 512]], off=(6 + p) * 512), in0=tb.ap([[1, 512]]), in1=gt.ap([[1, 512]], off=(6 + p) * 512), op=ALU.mult),
                           rd=[tb, gt], wr=[yT])
                    return f
                chunks.append(dT_chunk(0))
                chunks.append(dT_chunk(1))

                ob_B = PSB[7]
                rcB = Buf(rec.t, rec.rl, "recB", base=0)
                for j in range(4):
                    for rql in range(2):
                        rq = g * 8 + j * 2 + rql
                        rs_ = min(max(rq - 4, 0), 56)
                        tiles = list(range(rs_ // 2, (rs_ + 7) // 2 + 1))
                        nk = len(tiles)
                        m0 = 2 * tiles[0] - rq + 8
                        for h in range(4):
                            p, hl = h // 2, h % 2
                            sbk = PSB[nextps()]
                            for ki, i in enumerate(tiles):
                                OP("pe", lambda e, ki=ki, i=i, p=p, hl=hl, sbk=sbk, rq=rq: e.matmul(sbk.ap([[1, 64]], off=ki * 64), lhsT=kw.ap([[1, 128]], off=p * 1024 + (i - tlo) * 128),
                                                                                               rhs=qz[p].ap([[1, 64]], off=hl * 512 + (rq % 8) * 64), start=True, stop=True),
                                   rd=[kw, qz[p]], wr=[sbk])
                            ex = exb[rot("exb", 3)]
                            pt = ptb[rot("ptb", 3)]
                            OP("act", lambda e, sbk=sbk, ex=ex, nk=nk: e.activation(out=ex.ap([[1, nk * 64]]), in_=sbk.ap([[1, nk * 64]]), func=AF.Exp), rd=[sbk], wr=[ex])
                            OP("dve", lambda e, ex=ex, pt=pt, nk=nk, h=h, m0=m0: e.tensor_tensor(out=pt.ap([[64, nk], [1, 64]]), in0=ex.ap([[64, nk], [1, 64]]), in1=EB.ap([[128, nk], [1, 64]], off=h * 1024 + m0 * 64), op=ALU.mult),
                               rd=[ex, EB], wr=[pt])
                            if chunks:
                                chunks.pop(0)()
                            for ki, i in enumerate(tiles):
                                lo_ok = (2 * i >= rs_)
                                hi_ok = (2 * i + 1 <= rs_ + 7)
                                if not (lo_ok and hi_ok):
                                    z0 = 0 if hi_ok else 64
                                    OP("dve", lambda e, ki=ki, z0=z0, pt=pt: e.memset(pt.ap([[1, 64]], off=ki * 64, p0=z0, pn=64), 0.0), rd=[pt], wr=[pt])
                            for ki, i in enumerate(tiles):
                                OP("pe", lambda e, ki=ki, i=i, pt=pt, h=h, rql=rql, nk=nk: e.matmul(ob_B.ap([[1, 65]], off=h * 65, p0=rql * 64, pn=64), lhsT=pt.ap([[1, 64]], off=ki * 64),
                                                                                                rhs=vw.ap([[1, 65]], off=(i - tlo) * 260 + h * 65), start=(ki == 0), stop=(ki == nk - 1), tile_position=(0, rql * 64)),
                                   rd=[pt, vw], wr=[ob_B])
                    OP("dve", lambda e: e.reciprocal(out=rcB.ap([[1, 4]]), in_=ob_B.ap([[65, 4]], off=64)), rd=[ob_B], wr=[rcB], er=[rec.tok])
                    OP("dve", lambda e, j=j: e.tensor_tensor(out=obb.ap([[64, 4], [1, 64]], off=j * 256), in0=ob_B.ap([[65, 4], [1, 64]]), in1=rcB.ap([[1, 4], [0, 64]]), op=ALU.mult),
                       rd=[ob_B, rcB], wr=[obb], er=[rec.tok])
                while chunks:
                    chunks.pop(0)()
                for p in range(2):
                    tb = psb16[5 + p]
                    for j in range(4):
                        OP("pe", lambda e, j=j, p=p, tb=tb: e.transpose(out=tb.ap([[1, 128]], off=j * 128), in_=obb.ap([[1, 128]], off=j * 256 + p * 128), identity=identb.ap([[1, 128]])),
                           rd=[obb, identb], wr=[tb])
                    OP("dve", lambda e, p=p, tb=tb: e.tensor_tensor(out=yT.ap([[1, 512]], off=(2 + p) * 512), in0=tb.ap([[1, 512]]), in1=gt.ap([[1, 512]], off=(2 + p) * 512), op=ALU.mult),
                       rd=[tb, gt], wr=[yT])
                if g + 1 < NG:
                    loads_b(g + 1)
                for j in range(4):
                    t = g * 4 + j
                    xb = xbs[j]
                    if j == 3:
                        xsrc_ap = dap(xsrc, t * 128 * D, [[D, 128], [1, D]])
                        OP("sp", lambda e, xb=xb, xsrc_ap=xsrc_ap: e.dma_start(out=xb.ap([[1, D]]), in_=xsrc_ap), wr=[xb], er=[("xo", t)] if l > 0 else [], dma="L_" + xb.tok)
                    for n in range(2):
                        pb = PSB[nextps()]
                        for kc in range(8):
                            OP("pe", lambda e, kc=kc, j=j, n=n, pb=pb: e.matmul(pb.ap([[1, 512]]), lhsT=yT.ap([[1, 128]], off=kc * 512 + j * 128), rhs=Wob.ap([[1, 512]], off=kc * D + n * 512),
                                                                            start=(kc == 0), stop=(kc == 7)), rd=[yT, Wob], wr=[pb])
                        OP("dve", lambda e, n=n, pb=pb, xb=xb: e.tensor_tensor(out=xb.ap([[1, 512]], off=n * 512), in0=pb.ap([[1, 512]]), in1=xb.ap([[1, 512]], off=n * 512), op=ALU.add), rd=[pb, xb], wr=[xb])
                    OP("pool", lambda e, xb=xb, t=t: e.dma_start(out=dap(out_d, t * 128 * D, [[D, 128], [1, D]]), in_=xb.ap([[1, D]])), rd=[xb], ew=[("xo", t)], dma="S_" + xb.tok)

            loads_a(0)
            loads_b(0)
            for g in range(NG):
                do_group(g)

        S.op("sp", None, r=[("xo", t) for t in range(NT)])
        S.emit(nc, es)
    return nc


def _consts():
    bf = ml_dtypes.bfloat16
    identb = np.eye(128, dtype=np.float32).astype(bf)
    bo = np.zeros((128, 128), np.float32)
    bo[:64, :64] = 1.0 / 64
    bo[64:, 64:] = 1.0 / 64
    bo = bo.astype(bf)
    e = np.arange(64)
    ang = 2 * np.pi * np.outer(e, e) / 64.0
    c64 = np.concatenate([np.concatenate([np.cos(ang) / 512.0, -np.sin(ang) / 512.0], axis=1), np.zeros((64, 128))], axis=0).astype(np.float32).astype(bf)
    r1 = np.concatenate([np.concatenate([np.cos(ang), np.sin(ang)], axis=1), np.zeros((64, 128))], axis=0).astype(np.float32).astype(bf)
    s2 = np.arange(64)[:, None, None]
    k1 = np.arange(64)[None, :, None]
    k2 = np.arange(64)[None, None, :]
    th = 2 * np.pi * (((k1 + 64 * k2) * s2) % 4096) / 4096.0
    ct, st = np.cos(th), np.sin(th)
    r3 = np.zeros((2, 64, 64, 2, 64), np.float64)
    r3[0, :, :, 0, :] = ct
    r3[1, :, :, 0, :] = st
    r3[0, :, :, 1, :] = -st
    r3[1, :, :, 1, :] = ct
    r3 = r3.reshape(128, 8192).astype(np.float32).astype(bf)
    return identb, bo, c64, r1, r3


def _layer_params(inp):
    f = np.float32
    par = np.zeros((2, 128, NPAR), f)
    wst = np.zeros((2, 128, 512), f)
    fnw = np.zeros((2, 64, 256), f)
    biasT = np.zeros((2, 128, 4096), f)
    ck = np.arange(64)[:, None]
    cq = np.arange(64)[None, :]
    cs = np.clip(cq - 8, 0, 48)
    colok = (ck >= cs) & (ck < cs + 16)
    dcol = np.clip(ck - cq + 15, 0, 30)
    for l in range(2):
        par[l, :, 0:8] = inp["norm_g"][l].reshape(8, 128).T
        par[l, :, 8:16] = inp["mem_norm_g"][l].reshape(8, 128).T
        par[l, :, 16] = np.tile(inp["na_qn_g"][l], 2)
        par[l, :, 17] = np.tile(inp["na_kn_g"][l], 2)
        par[l, :, 18] = np.tile(inp["mem_qn_g"][l], 2)
        par[l, :, 19] = np.tile(inp["mem_kn_g"][l], 2)
        par[l, :, 20:276] = np.broadcast_to(inp["gm_ln_g"][l][None, :], (128, 256))
        par[l, :, 276:532] = np.broadcast_to(inp["gm_ln_b"][l][None, :], (128, 256))
        bs = inp["gm_b_s"][l]
        par[l, :, 532:788] = np.repeat(bs.reshape(2, 2, 1, 128), 64, axis=2).transpose(1, 2, 0, 3).reshape(128, 256)
        wst[l] = inp["gm_w_s"][l].transpose(2, 0, 1).reshape(128, 512)
        fnw[l] = inp["fn_w"][l].transpose(1, 0, 2).reshape(64, 256)
        rpb = inp["na_rpb"][l]
        bt = np.full((2, 64, 4, 16, 64), -30000.0, f)
        for rkl in range(2):
            for m in range(16):
                dr = m - 8 + rkl
                if -7 <= dr <= 7:
                    vals = rpb[:, dr + 7, :][:, dcol]
                    bt[rkl, :, :, m, :] = np.where(colok[None], vals, f(-30000.0)).transpose(1, 0, 2)
        biasT[l] = bt.reshape(128, 4096)
    return par, wst, fnw, biasT


_CACHE = {}


def _in_maps(inp):
    identb, bo, c64, r1, r3 = _consts()
    par, wst, fnw, biasT = _layer_params(inp)
    common = {
        "w_in": np.ascontiguousarray(inp["w_in"], np.float32), "w_out": np.ascontiguousarray(inp["w_out"], np.float32),
        "w_kv": np.ascontiguousarray(inp["mem_w_kv"], np.float32), "par": par, "wst": wst, "fnw": fnw, "biasT": biasT,
        "identb": identb, "bo": bo, "c64": c64, "r1": r1, "r3": r3,
    }
    maps = []
    for b in range(8):
        m = dict(common)
        m["x"] = np.ascontiguousarray(inp["x"][b], np.float32)
        m["mem"] = np.ascontiguousarray(inp["mem"][b], np.float32)
        maps.append(m)
    return maps


def kernel(**inputs):
    inp = {k: np.asarray(v) for k, v in inputs.items()}
    if "nc" not in _CACHE:
        _CACHE["nc"] = build()
    res = run_bass_kernel_spmd(_CACHE["nc"], _in_maps(inp), core_ids=list(range(8)))
    return np.stack([np.asarray(r["out"], np.float32) for r in res.results], axis=0)
```

```python
from contextlib import ExitStack
import numpy as np
import ml_dtypes
import concourse.bass as bass
import concourse.mybir as mybir
from concourse.bass_utils import run_bass_kernel_spmd

F32 = mybir.dt.float32
BF16 = mybir.dt.bfloat16
AF = mybir.ActivationFunctionType
ALU = mybir.AluOpType

D = 1024
SEQ = 4096
NT = 32
NG = 8
INC = 2816
EPS = 1e-6
NPAR = 788
REORDER = True
PE_FREE = (1, 2, 3)
C_AU, C_AV, C_AG, C_BQ, C_BK, C_BV, C_BG, C_CI, C_CG, C_DQ, C_DG = [i * 256 for i in range(11)]


class _FakeIns:
    def then_inc(self, *a, **k):
        return self


class _FakeEng:
    def __init__(self):
        self.calls = []

    def __getattr__(self, name):
        def f(*a, **k):
            self.calls.append((name, a, k))
            return _FakeIns()
        return f


def _free(ap):
    n = 1
    for d in ap.shape[1:]:
        n *= d
    return n


def _estimate_cost(eng, fn, dma):
    if fn is None:
        return 0.0
    fe = _FakeEng()
    try:
        fn(fe)
        name, a, k = fe.calls[0]
    except Exception:
        name, a, k = "?", (), {}
    try:
        if dma is not None:
            src = k.get("in_")
            nbytes = _free(src) * src.shape[0] * (2 if src.dtype == BF16 else 4)
            return 2.0 + nbytes / 200e3
        if eng == "pe":
            if name == "matmul":
                n = _free(k["rhs"])
            else:
                n = _free(k["in_"])
            return max(n, 96) / 1350.0 + 0.04
        out = k.get("out", a[0] if a else None)
        n = _free(out)
        if eng == "act":
            return 0.25 + n / 1200.0
        if name == "reciprocal":
            return 0.08 + n * 6.0 / 960.0
        return 0.08 + n / 960.0
    except Exception:
        return 4.0 if dma is not None else {"pe": 0.2, "act": 0.6, "dve": 0.5, "pool": 0.5, "sp": 0.05}[eng]


class _Ins:
    __slots__ = ("eng", "fn", "deps", "dma", "dma_val", "needs_sig", "sig", "idx", "cost", "pos", "waits", "wtok", "phase")


class Sched:
    ENGS = ("pe", "act", "dve", "pool", "sp")
    LAT = 1.2
    FIXED = ("pool",)

    def __init__(self, reorder=True):
        self.ins = []
        self.lastw = {}
        self.readers = {}
        self.dma_cnt = {}
        self.reorder = reorder
        self.phase = 0
        self.pe_free = set(PE_FREE)

    def op(self, eng, fn, r=(), w=(), dma=None, cost=None):
        i = len(self.ins)
        deps = set()
        for t in r:
            if t in self.lastw:
                deps.add(self.lastw[t])
        for t in w:
            if t in self.lastw:
                deps.add(self.lastw[t])
            deps.update(self.readers.get(t, ()))
        rec = _Ins()
        rec.eng, rec.fn, rec.dma, rec.idx = eng, fn, dma, i
        rec.needs_sig = False
        rec.sig = None
        rec.dma_val = None
        if cost is None:
            cost = _estimate_cost(eng, fn, dma)
        rec.cost = cost
        if dma is not None:
            self.dma_cnt[dma] = self.dma_cnt.get(dma, 0) + 16
            rec.dma_val = self.dma_cnt[dma]
        rec.deps = sorted(deps)
        rec.wtok = list(w)
        rec.phase = self.phase
        self.ins.append(rec)
        for t in r:
            self.readers.setdefault(t, []).append(i)
        for t in w:
            self.lastw[t] = i
            self.readers[t] = []
        return i

    def _schedule(self):
        import heapq
        ins = self.ins
        n = len(ins)
        order = {e: [] for e in self.ENGS}
        if not self.reorder:
            for rec in ins:
                order[rec.eng].append(rec.idx)
            return order
        gid = [0] * n
        members = []
        last_pe = None
        for rec in ins:
            key = None
            if rec.eng == "pe" and rec.dma is None and rec.fn is not None:
                key = tuple(rec.wtok)
            if key is not None and last_pe is not None and last_pe[0] == key:
                g = last_pe[1]
                members[g].append(rec.idx)
            else:
                g = len(members)
                members.append([rec.idx])
            if rec.eng == "pe":
                last_pe = (key, g)
            gid[rec.idx] = g
        ng = len(members)
        ndeps = [set() for _ in range(ng)]
        prev = {}
        for rec in ins:
            g = gid[rec.idx]
            for j in rec.deps:
                if gid[j] != g:
                    ndeps[g].add(gid[j])
            if rec.eng in self.FIXED or rec.eng == "pe":
                fixed = rec.eng in self.FIXED or rec.phase not in self.pe_free
                if fixed and rec.eng in prev and prev[rec.eng] != g:
                    ndeps[g].add(prev[rec.eng])
                prev[rec.eng] = g
        succ = [[] for _ in range(ng)]
        npred = [len(d) for d in ndeps]
        for g, d in enumerate(ndeps):
            for j in d:
                succ[j].append(g)
        geng = [ins[m[0]].eng for m in members]
        gcost = [sum(ins[i].cost for i in m) for m in members]
        blev = [0.0] * ng
        for g in range(ng - 1, -1, -1):
            m = 0.0
            for k in succ[g]:
                if blev[k] > m:
                    m = blev[k]
            blev[g] = gcost[g] + (m + self.LAT if succ[g] else 0.0)
        fin = [0.0] * ng
        ready_t = [0.0] * ng
        later = {e: [] for e in self.ENGS}
        now = {e: [] for e in self.ENGS}
        free = {e: 0.0 for e in self.ENGS}
        for g in range(ng):
            if npred[g] == 0:
                heapq.heappush(later[geng[g]], (0.0, g))
        done = 0
        while done < ng:
            best = None
            for e in self.ENGS:
                lt, nw = later[e], now[e]
                while lt and lt[0][0] <= free[e]:
                    k_ = heapq.heappop(lt)[1]
                    heapq.heappush(nw, (-blev[k_], k_))
                if nw:
                    cand = (free[e], nw[0][1], e, True)
                elif lt:
                    cand = (lt[0][0], lt[0][1], e, False)
                else:
                    continue
                if best is None or cand[:2] < best[:2]:
                    best = cand
            st, g, e, from_now = best
            if from_now:
                heapq.heappop(now[e])
            else:
                heapq.heappop(later[e])
            rec = ins[members[g][0]]
            order[e].extend(members[g])
            if rec.dma is not None:
                free[e] = st + 0.06
                fin[g] = st + gcost[g]
            elif rec.fn is None:
                free[e] = st
                fin[g] = st
            else:
                free[e] = st + gcost[g]
                fin[g] = st + gcost[g]
            done += 1
            for k in succ[g]:
                npred[k] -= 1
                t = fin[g] + (0.0 if (geng[k] == e == "pe" and rec.dma is None) else self.LAT)
                if t > ready_t[k]:
                    ready_t[k] = t
                if npred[k] == 0:
                    heapq.heappush(later[geng[k]], (ready_t[k], k))
        return order

    def plan(self):
        ins = self.ins
        order = self._schedule()
        for e in self.ENGS:
            for pos, i in enumerate(order[e]):
                ins[i].pos = pos
        for rec in ins:
            best = {}
            for j in rec.deps:
                d = ins[j]
                if d.dma is not None:
                    k = ("dma", d.dma)
                    v = d.dma_val
                else:
                    if d.eng == "pe" and rec.eng == "pe" and rec.dma is None:
                        assert d.pos < rec.pos
                        continue
                    k = d.eng
                    v = d.pos
                if k not in best or best[k][0] < v:
                    best[k] = (v, j)
            rec.waits = [j for (_, j) in best.values()]
            for j in rec.waits:
                if ins[j].dma is None:
                    ins[j].needs_sig = True
        cnt = {e: 0 for e in self.ENGS}
        for e in self.ENGS:
            for i in order[e]:
                rec = ins[i]
                if rec.needs_sig and rec.dma is None:
                    cnt[e] += 1
                    rec.sig = cnt[e]
        self._check(order)
        return order

    def _check(self, order):
        ins = self.ins
        ptr = {e: 0 for e in self.ENGS}
        executed = [False] * len(ins)
        total = sum(len(v) for v in order.values())
        n = 0
        progress = True
        while progress:
            progress = False
            for e in self.ENGS:
                while ptr[e] < len(order[e]):
                    rec = ins[order[e][ptr[e]]]
                    if all(executed[j] for j in rec.waits):
                        executed[rec.idx] = True
                        ptr[e] += 1
                        n += 1
                        progress = True
                    else:
                        break
        assert n == total, "schedule deadlocks (%d of %d)" % (n, total)

    def emit(self, nc, es):
        order = self.plan()
        engsem = {e: es.enter_context(nc.semaphore("s_" + e)) for e in self.ENGS}
        dmasem = {}
        for k in self.dma_cnt:
            dmasem[k] = es.enter_context(nc.semaphore("d_%d" % len(dmasem)))
        ins = self.ins

        def run(eng_name, eng):
            waited = {}
            for i in order[eng_name]:
                rec = ins[i]
                for j in rec.waits:
                    d = ins[j]
                    if d.dma is not None:
                        sem, val, sk = dmasem[d.dma], d.dma_val, ("dma", d.dma)
                    else:
                        sem, val, sk = engsem[d.eng], d.sig, d.eng
                    if waited.get(sk, 0) >= val:
                        continue
                    waited[sk] = val
                    eng.wait_ge(sem, val)
                if rec.fn is None:
                    continue
                bi = rec.fn(eng)
                if rec.dma is not None:
                    bi.then_inc(dmasem[rec.dma], 16)
                elif rec.needs_sig:
                    bi.then_inc(engsem[eng_name], 1)

        with nc.Block() as block:
            @block.tensor
            def _(e):
                run("pe", e)

            @block.scalar
            def _(e):
                run("act", e)

            @block.vector
            def _(e):
                run("dve", e)

            @block.gpsimd
            def _(e):
                run("pool", e)

            @block.sync
            def _(e):
                run("sp", e)


class Buf:
    def __init__(self, t, rl, tok, base=0, ar=None):
        self.t, self.rl, self.tok, self.base, self.ar = t, rl, tok, base, ar

    def ap(self, dims, off=0, p0=0, pn=128):
        return bass.AP(self.t, p0 * self.rl + self.base + off, [[self.rl, pn]] + [list(d) for d in dims])


def dap(t, off, dims):
    return bass.AP(t, off, [list(d) for d in dims])


def build(nlayers=2, debug=False):
    nc = bass.Bass("TRN2", target_bir_lowering=False)
    nc.allow_low_precision("bf16 matmul operands with fp32 PSUM accumulation")
    IN = lambda n, s, d=F32: nc.dram_tensor(n, s, d, kind="ExternalInput")
    x_d = IN("x", [SEQ, D])
    mem_d = IN("mem", [256, D])
    win_d = IN("w_in", [2, D, INC])
    wout_d = IN("w_out", [2, D, D])
    wkv_d = IN("w_kv", [2, D, 512])
    par_d = IN("par", [2, 128, NPAR])
    wst_d = IN("wst", [2, 128, 512])
    fnw_d = IN("fnw", [2, 64, 256])
    bias_d = IN("biasT", [2, 128, 4096])
    identb_d = IN("identb", [128, 128], BF16)
    bo_d = IN("bo", [128, 128], BF16)
    c64_d = IN("c64", [128, 128], BF16)
    r1_d = IN("r1", [128, 128], BF16)
    r3_d = IN("r3", [128, 8192], BF16)
    out_d = nc.dram_tensor("out", [SEQ, D], F32, kind="ExternalOutput")
    sk = "ExternalOutput" if debug else "Internal"
    hT_d = nc.dram_tensor("hT_s", [NG, 128, 4096], BF16, kind=sk)
    nrm_d = nc.dram_tensor("nrm_s", [6, 128, 4096], BF16, kind=sk)
    va_d = nc.dram_tensor("va_s", [128, NT * 260], BF16, kind=sk)
    cin_d = nc.dram_tensor("cin_s", [2, 128, 4096], BF16, kind=sk)
    zz_d = nc.dram_tensor("zz_s", [2, 128, 4096], BF16, kind=sk)

    S = Sched(reorder=REORDER)
    with ExitStack() as es:
        def sb(name, rl, dt, pn=128):
            return Buf(es.enter_context(nc.sbuf_tensor("sb_" + name, [pn, rl], dt)), rl, name)

        Wb = sb("Wb", 8 * INC, BF16)
        Wob = sb("Wob", 8 * D, BF16)
        par = sb("par", NPAR, F32)
        gg = sb("gg", 2, F32)
        wsb = sb("wsb", 512, BF16)
        ABh = sb("ABh", 512, BF16)
        EB = sb("EB", 4096, BF16)
        kmn = sb("kmn", 512, BF16)
        kmnz = sb("kmnz", 1024, BF16)
        Vm = sb("Vm", 2 * 260, BF16)
        identb = sb("identb", 128, BF16)
        bo = sb("bo", 128, BF16)
        c64 = sb("c64", 128, BF16)
        r1 = sb("r1", 128, BF16)
        wst = [sb("wstg%d" % i, 2048, F32) for i in range(2)]
        xs = [sb("xs%d" % i, D, F32) for i in range(3)]
        hs = [sb("hs%d" % i, D, BF16) for i in range(3)]
        st1 = [sb("st1_%d" % i, 8, F32) for i in range(2)]
        hTg = [sb("hTg%d" % i, 8 * 512, BF16) for i in range(2)]
        sqb = [sb("sqb%d" % i, 512, F32) for i in range(2)]
        sq16 = [sb("sq16_%d" % i, 512, BF16) for i in range(4)]
        rb = [sb("rb%d" % i, 512, F32) for i in range(4)]
        nst = [sb("nst%d" % i, 6 * 512, BF16) for i in range(2)]
        vst = [sb("vst%d" % i, 1040, BF16) for i in range(2)]
        cst = [sb("cst%d" % i, 2 * 512, BF16) for i in range(2)]
        arA = sb("arA", 8192, BF16)
        arB = sb("arB", 8192, BF16)
        Wkvb = arB
        r3b = [sb("r3b%d" % i, 1024, BF16) for i in range(2)]
        cinh = hTg[0]
        zs = hTg[1]
        PSB = []
        for i in range(8):
            PSB.append(Buf(es.enter_context(nc.psum_tensor("ps%d" % i, [128, 512], F32)), 512, "ps%d" % i))
        psb16 = [Buf(PSB[i].t.bitcast(BF16), 1024, PSB[i].tok) for i in range(8)]
        rr = {"ps": 0}

        def nextps():
            i = rr["ps"]
            rr["ps"] = (i + 1) % 5
            return i

        rr["ph"] = 0

        def nextph():
            i = rr["ph"]
            rr["ph"] = (i + 1) % 3
            return 5 + i

        def OP(eng, fn, rd=(), wr=(), dma=None, er=(), ew=()):
            r = list(er)
            w = list(ew)
            for b in rd:
                r.append(b.tok)
                if b.ar:
                    r.append(b.ar)
            for b in wr:
                w.append(b.tok)
                if b.ar:
                    r.append(b.ar)
            S.op(eng, fn, r=r, w=w, dma=dma)

        WbA = Buf(Wb.t, Wb.rl, "WbA")
        WbB = Buf(Wb.t, Wb.rl, "WbB")
        parS = Buf(par.t, par.rl, "parS")
        parL = Buf(par.t, par.rl, "parL")
        cnt = {}

        def rot(name, n):
            c = cnt.get(name, 0)
            cnt[name] = c + 1
            return c % n

        def dma_in(dst, dst_ap, src_ap, q="sp", extra_w=()):
            S.op(q, lambda e: e.dma_start(out=dst_ap, in_=src_ap), w=[dst.tok] + list(extra_w), dma="L_" + dst.tok)

        dma_in(identb, identb.ap([[1, 128]]), dap(identb_d, 0, [[128, 128], [1, 128]]))
        dma_in(bo, bo.ap([[1, 128]]), dap(bo_d, 0, [[128, 128], [1, 128]]))
        dma_in(c64, c64.ap([[1, 128]]), dap(c64_d, 0, [[128, 128], [1, 128]]))
        dma_in(r1, r1.ap([[1, 128]]), dap(r1_d, 0, [[128, 128], [1, 128]]))
        S.op("dve", lambda e: e.memset(Vm.ap([[1, 520]]), 1.0), w=[Vm.tok])

        def rstd_from_ssq(st, col, scale):
            a = st.ap([[1, 1]], off=col)
            S.op("act", lambda e: e.activation(out=a, in_=a, func=AF.Ln, scale=scale, bias=EPS), r=[st.tok], w=[st.tok])
            S.op("act", lambda e: e.activation(out=a, in_=a, func=AF.Exp, scale=-0.5), r=[st.tok], w=[st.tok])

        def norm_transpose(xbuf, dst, dst_off, dst_kstride):
            h = hs[rot("hs", 3)]
            st = st1[rot("st1", 2)]
            S.op("act", lambda e: e.activation(out=h.ap([[1, D]]), in_=xbuf.ap([[1, D]]), func=AF.Square, accum_out=st.ap([[1, 1]])),
                 r=[xbuf.tok], w=[h.tok, st.tok])
            rstd_from_ssq(st, 0, 1.0 / D)
            S.op("dve", lambda e: e.tensor_scalar(out=h.ap([[1, D]]), in0=xbuf.ap([[1, D]]), scalar1=st.ap([[1, 1]]), scalar2=None, op0=ALU.mult),
                 r=[xbuf.tok, st.tok], w=[h.tok])
            pi = nextps()
            pb = psb16[pi]
            for kc in range(8):
                S.op("pe", lambda e, kc=kc: e.transpose(out=pb.ap([[1, 128]], off=kc * 128), in_=h.ap([[1, 128]], off=kc * 128), identity=identb.ap([[1, 128]])),
                     r=[h.tok, identb.tok], w=[pb.tok])
            S.op("dve", lambda e: e.tensor_copy(out=dst.ap([[dst_kstride, 8], [1, 128]], off=dst_off), in_=pb.ap([[128, 8], [1, 128]])),
                 r=[pb.tok], w=[dst.tok])

        def proj_fm(hT, col, n=512, W=None, wrow=INC):
            W = W or WbA
            pi = nextps()
            pb = PSB[pi]
            for kc in range(8):
                S.op("pe", lambda e, kc=kc: e.matmul(pb.ap([[1, n]]), lhsT=W.ap([[1, 128]], off=kc * wrow + col), rhs=hT.ap([[1, n]], off=kc * n),
                                                      start=(kc == 0), stop=(kc == 7)),
                     r=[W.tok, hT.tok], w=[pb.tok])
            return pb

        def headnorm(pb, n, gcol, out_buf, out_ap):
            sq = sq16[rot("sq16", 4)]
            r_ = rb[rot("rb", 4)]
            S.op("act", lambda e: e.activation(out=sq.ap([[1, n]]), in_=pb.ap([[1, n]]), func=AF.Square), r=[pb.tok], w=[sq.tok])
            p2 = PSB[nextps()]
            S.op("pe", lambda e: e.matmul(p2.ap([[1, n]]), lhsT=bo.ap([[1, 128]]), rhs=sq.ap([[1, n]]), start=True, stop=True),
                 r=[bo.tok, sq.tok], w=[p2.tok])
            S.op("act", lambda e: e.activation(out=r_.ap([[1, n]]), in_=p2.ap([[1, n]]), func=AF.Ln, bias=EPS), r=[p2.tok], w=[r_.tok])
            S.op("act", lambda e: e.activation(out=r_.ap([[1, n]]), in_=r_.ap([[1, n]]), func=AF.Exp, scale=-0.5), r=[r_.tok], w=[r_.tok])
            sc = gcol if gcol is not None else 1.0
            rd = [pb.tok, r_.tok] + ([gg.tok] if gcol is not None else [])
            S.op("dve", lambda e: e.scalar_tensor_tensor(out=out_ap, in0=pb.ap([[1, n]]), scalar=sc, in1=r_.ap([[1, n]]), op0=ALU.mult, op1=ALU.mult),
                 r=rd, w=[out_buf.tok])

        for l in range(nlayers):
            xsrc = x_d if l == 0 else out_d
            def p1(g):
                hT = hTg[g % 2]
                for j in range(4):
                    t = g * 4 + j
                    xb = xs[rot("xs", 3)]
                    xsrc_ap = dap(xsrc, t * 128 * D, [[D, 128], [1, D]])
                    S.op("sp", lambda e, xb=xb, xsrc_ap=xsrc_ap: e.dma_start(out=xb.ap([[1, D]]), in_=xsrc_ap),
                         r=[("xo", t)] if l > 0 else [], w=[xb.tok], dma="L_" + xb.tok)
                    norm_transpose(xb, hT, j * 128, 512)
                S.op("pool", lambda e, hT=hT, g=g: e.dma_start(out=dap(hT_d, g * 128 * 4096, [[4096, 128], [1, 4096]]), in_=hT.ap([[1, 4096]])),
                     r=[hT.tok], w=[("hT_d", g)], dma="S_" + hT.tok)

            def p2(g):
                hT = hTg[g % 2]
                ns = nst[rot("nst", 2)]
                for si, (col, gcol) in enumerate(((C_BK, None), (C_BQ, gg.ap([[1, 1]], off=0)), (C_DQ, gg.ap([[1, 1]], off=1)))):
                    for p in range(2):
                        pb = proj_fm(hT, col + p * 128)
                        headnorm(pb, 512, gcol, ns, ns.ap([[1, 512]], off=(si * 2 + p) * 512))
                S.op("pool", lambda e, ns=ns, g=g: e.dma_start(out=dap(nrm_d, g * 512, [[4096, 128], [128 * 4096, 6], [1, 512]]), in_=ns.ap([[512, 6], [1, 512]])),
                     r=[ns.tok], w=[("nrm_d", g)], dma="S_" + ns.tok)
                vs = vst[rot("vst", 2)]
                for j in range(4):
                    pb = PSB[nextps()]
                    for kc in range(8):
                        S.op("pe", lambda e, kc=kc, j=j, pb=pb, hT=hT: e.matmul(pb.ap([[1, 256]]), lhsT=hT.ap([[1, 128]], off=kc * 512 + j * 128), rhs=Wb.ap([[1, 256]], off=kc * INC + C_BV),
                                                                           start=(kc == 0), stop=(kc == 7)),
                             r=[hT.tok, WbA.tok], w=[pb.tok])
                    S.op("dve", lambda e, j=j, pb=pb, vs=vs: e.tensor_copy(out=vs.ap([[65, 4], [1, 64]], off=j * 260), in_=pb.ap([[64, 4], [1, 64]])), r=[pb.tok], w=[vs.tok])
                S.op("pool", lambda e, vs=vs, g=g: e.dma_start(out=dap(va_d, g * 1040, [[NT * 260, 128], [1, 1040]]), in_=vs.ap([[1, 1040]])),
                     r=[vs.tok], w=[("va_d", g)], dma="S_" + vs.tok)
                cs = cst[rot("cst", 2)]
                for p in range(2):
                    pb = proj_fm(hT, C_CI + p * 128)
                    S.op("act", lambda e, p=p, pb=pb, cs=cs: e.activation(out=cs.ap([[1, 512]], off=p * 512), in_=pb.ap([[1, 512]]), func=AF.Copy), r=[pb.tok], w=[cs.tok])
                S.op("pool", lambda e, cs=cs, g=g: e.dma_start(out=dap(cin_d, g * 512, [[4096, 128], [128 * 4096, 2], [1, 512]]), in_=cs.ap([[512, 2], [1, 512]])),
                     r=[cs.tok], w=[("cin_d", g)], dma="S_" + cs.tok)

            S.phase = 1
            p1(0)
            p1(1)
            S.phase = 0
            for v in vst:
                S.op("dve", lambda e, v=v: e.memset(v.ap([[1, 1040]]), 1.0), w=[v.tok])
            dma_in(parS, parS.ap([[1, 20]]), dap(par_d, l * 128 * NPAR, [[NPAR, 128], [1, 20]]))
            for kc in range(8):
                w_ = wst[rot("wst", 2)]
                dma_in(w_, w_.ap([[1, 768]]), dap(win_d, (l * D + kc * 128) * INC + 768, [[INC, 128], [1, 768]]))
                S.op("act", lambda e, kc=kc, w_=w_: e.activation(out=Wb.ap([[1, 768]], off=kc * INC + 768), in_=w_.ap([[1, 768]]), func=AF.Copy, scale=parS.ap([[1, 1]], off=kc)),
                     r=[w_.tok, parS.tok], w=[WbA.tok])
                w_ = wst[rot("wst", 2)]
                dma_in(w_, w_.ap([[256, 2], [1, 256]]), dap(win_d, (l * D + kc * 128) * INC + 1792, [[INC, 128], [512, 2], [1, 256]]))
                S.op("act", lambda e, kc=kc, w_=w_: e.activation(out=Wb.ap([[512, 2], [1, 256]], off=kc * INC + 1792), in_=w_.ap([[256, 2], [1, 256]]), func=AF.Copy, scale=parS.ap([[1, 1]], off=kc)),
                     r=[w_.tok, parS.tok], w=[WbA.tok])
            for j, c0 in ((0, 16), (1, 18)):
                S.op("dve", lambda e, j=j, c0=c0: e.scalar_tensor_tensor(out=gg.ap([[1, 1]], off=j), in0=parS.ap([[1, 1]], off=c0), scalar=0.125, in1=parS.ap([[1, 1]], off=c0 + 1), op0=ALU.mult, op1=ALU.mult),
                     r=[parS.tok], w=[gg.tok])
            S.phase = 1
            for g in range(NG):
                p2(g)
                if g + 2 < NG:
                    p1(g + 2)

            S.phase = 0
            dma_in(parL, parL.ap([[1, NPAR - 20]], off=20), dap(par_d, l * 128 * NPAR + 20, [[NPAR, 128], [1, NPAR - 20]]))
            for kc in range(8):
                w_ = wst[rot("wst", 2)]
                dma_in(w_, w_.ap([[1, 768]]), dap(win_d, (l * D + kc * 128) * INC + 0, [[INC, 128], [1, 768]]))
                S.op("act", lambda e, kc=kc, w_=w_: e.activation(out=Wb.ap([[1, 768]], off=kc * INC + 0), in_=w_.ap([[1, 768]]), func=AF.Copy, scale=parS.ap([[1, 1]], off=kc)),
                     r=[w_.tok, parS.tok], w=[WbB.tok])
                w_ = wst[rot("wst", 2)]
                dma_in(w_, w_.ap([[256, 3], [1, 256]]), dap(win_d, (l * D + kc * 128) * INC + 1536, [[INC, 128], [512, 3], [1, 256]]))
                S.op("act", lambda e, kc=kc, w_=w_: e.activation(out=Wb.ap([[512, 3], [1, 256]], off=kc * INC + 1536), in_=w_.ap([[256, 3], [1, 256]]), func=AF.Copy, scale=parS.ap([[1, 1]], off=kc)),
                     r=[w_.tok, parS.tok], w=[WbB.tok])
            for kc in range(8):
                w_ = wst[rot("wst", 2)]
                dma_in(w_, w_.ap([[1, D]]), dap(wout_d, (l * D + kc * 128) * D, [[D, 128], [1, D]]))
                S.op("dve", lambda e, kc=kc, w_=w_: e.tensor_scalar(out=Wob.ap([[1, D]], off=kc * D), in0=w_.ap([[1, D]]), scalar1=0.5, scalar2=None, op0=ALU.mult),
                     r=[w_.tok], w=[Wob.tok])
            for kc in range(8):
                w_ = wst[rot("wst", 2)]
                dma_in(w_, w_.ap([[1, 512]]), dap(wkv_d, (l * D + kc * 128) * 512, [[512, 128], [1, 512]]))
                S.op("act", lambda e, kc=kc, w_=w_: e.activation(out=Wkvb.ap([[1, 512]], off=kc * 512), in_=w_.ap([[1, 512]]), func=AF.Copy, scale=parS.ap([[1, 1]], off=8 + kc)),
                     r=[w_.tok, parS.tok], w=[Wkvb.tok])
            w_ = wst[rot("wst", 2)]
            dma_in(w_, w_.ap([[1, 512]]), dap(wst_d, l * 128 * 512, [[512, 128], [1, 512]]))
            S.op("dve", lambda e, w_=w_: e.tensor_copy(out=wsb.ap([[1, 512]]), in_=w_.ap([[1, 512]])), r=[w_.tok], w=[wsb.tok])
            for hh in range(2):
                w_ = wst[rot("wst", 2)]
                dma_in(w_, w_.ap([[1, 2048]]), dap(bias_d, l * 128 * 4096 + hh * 2048, [[4096, 128], [1, 2048]]))
                S.op("act", lambda e, hh=hh, w_=w_: e.activation(out=EB.ap([[1, 2048]], off=hh * 2048), in_=w_.ap([[1, 2048]]), func=AF.Exp), r=[w_.tok], w=[EB.tok])
            w_ = wst[rot("wst", 2)]
            dma_in(w_, w_.ap([[1, 256]], pn=64), dap(fnw_d, l * 64 * 256, [[256, 64], [1, 256]]))
            fb = sq16[rot("sq16", 4)]
            S.op("dve", lambda e, fb=fb: e.memset(fb.ap([[1, 256]]), 0.0), w=[fb.tok])
            S.op("dve", lambda e, w_=w_, fb=fb: e.tensor_copy(out=fb.ap([[1, 256]], pn=64), in_=w_.ap([[1, 256]], pn=64)), r=[w_.tok, fb.tok], w=[fb.tok])
            pb = PSB[nextps()]
            for h in range(4):
                for wv in range(2):
                    S.op("pe", lambda e, h=h, wv=wv, fb=fb, pb=pb: e.matmul(pb.ap([[1, 64]], off=h * 128 + wv * 64, p0=(h % 2) * 64, pn=64), lhsT=c64.ap([[1, 64]], off=wv * 64),
                                                                         rhs=fb.ap([[1, 64]], off=h * 64), start=True, stop=True, tile_position=(0, (h % 2) * 64)),
                         r=[c64.tok, fb.tok], w=[pb.tok])
            S.op("dve", lambda e: e.memset(ABh.ap([[1, 512]]), 0.0), w=[ABh.tok])
            for hl in range(2):
                S.op("dve", lambda e, pb=pb, hl=hl: e.tensor_copy(out=ABh.ap([[256, 2], [1, 128]], off=hl * 128, p0=hl * 64, pn=64), in_=pb.ap([[256, 2], [1, 128]], off=hl * 128, p0=hl * 64, pn=64)),
                     r=[pb.tok, ABh.tok], w=[ABh.tok])
            memT = arA
            for mt in range(2):
                xb = xs[rot("xs", 3)]
                dma_in(xb, xb.ap([[1, D]]), dap(mem_d, mt * 128 * D, [[D, 128], [1, D]]))
                norm_transpose(xb, memT, mt * 128, 256)
            for p in range(2):
                pb = proj_fm(memT, p * 128, n=256, W=Wkvb, wrow=512)
                headnorm(pb, 256, None, kmn, kmn.ap([[1, 256]], off=p * 256))
            S.op("dve", lambda e: e.memset(kmnz.ap([[1, 1024]]), 0.0), w=[kmnz.tok])
            for hl in range(2):
                S.op("dve", lambda e, hl=hl: e.tensor_copy(out=kmnz.ap([[512, 2], [1, 256]], off=hl * 256, p0=hl * 64, pn=64), in_=kmn.ap([[256, 2], [1, 256]], p0=hl * 64, pn=64)),
                     r=[kmn.tok, kmnz.tok], w=[kmnz.tok])
            for mt in range(2):
                pb = PSB[nextps()]
                for kc in range(8):
                    S.op("pe", lambda e, kc=kc, mt=mt, pb=pb: e.matmul(pb.ap([[1, 256]]), lhsT=memT.ap([[1, 128]], off=kc * 256 + mt * 128), rhs=Wkvb.ap([[1, 256]], off=kc * 512 + 256),
                                                                  start=(kc == 0), stop=(kc == 7)),
                         r=[memT.tok, Wkvb.tok], w=[pb.tok])
                S.op("dve", lambda e, mt=mt, pb=pb: e.tensor_copy(out=Vm.ap([[65, 4], [1, 64]], off=mt * 260), in_=pb.ap([[64, 4], [1, 64]])), r=[pb.tok], w=[Vm.tok])

            S.phase = 2
            GG, PP = arA, arB
            for h in range(4):
                p, hl = h // 2, h % 2
                if hl == 0:
                    S.op("sp", lambda e, p=p: e.dma_start(out=cinh.ap([[1, 4096]]), in_=dap(cin_d, p * 128 * 4096, [[4096, 128], [1, 4096]])),
                         r=[("cin_d", g_) for g_ in range(NG)], w=[cinh.tok], dma="L_" + cinh.tok)
                    S.op("dve", lambda e: e.memset(GG.ap([[1, 8192]], p0=64, pn=64), 0.0), w=[GG.tok])
                for sb_ in range(16):
                    pb = PSB[nextps()]
                    for i in range(4):
                        s2 = sb_ * 4 + i
                        S.op("pe", lambda e, i=i, s2=s2, pb=pb, h=h: e.matmul(pb.ap([[1, 128]], off=i * 128, pn=64), lhsT=cinh.ap([[64, 64]], off=s2),
                                                                        rhs=ABh.ap([[1, 128]], off=h * 128), start=True, stop=True),
                             r=[cinh.tok, ABh.tok], w=[pb.tok])
                    S.op("act" if sb_ % 2 else "dve",
                         (lambda e, pb=pb, sb_=sb_: e.activation(out=GG.ap([[64, 4], [4096, 2], [1, 64]], off=sb_ * 256, pn=64), in_=pb.ap([[128, 4], [64, 2], [1, 64]], pn=64), func=AF.Copy)) if sb_ % 2 else
                         (lambda e, pb=pb, sb_=sb_: e.tensor_copy(out=GG.ap([[64, 4], [4096, 2], [1, 64]], off=sb_ * 256, pn=64), in_=pb.ap([[128, 4], [64, 2], [1, 64]], pn=64))),
                         r=[pb.tok], w=[GG.tok])
                for eb in range(16):
                    pb = PSB[nextps()]
                    for i in range(4):
                        e_ = eb * 4 + i
                        S.op("pe", lambda e, i=i, e_=e_, pb=pb: e.matmul(pb.ap([[1, 128]], off=i * 128), lhsT=GG.ap([[64, 128]], off=e_), rhs=r1.ap([[1, 128]]), start=True, stop=True),
                             r=[GG.tok, r1.tok], w=[pb.tok])
                    S.op("act" if eb % 2 else "dve",
                         (lambda e, pb=pb, eb=eb: e.activation(out=PP.ap([[1, 512]], off=eb * 512), in_=pb.ap([[1, 512]]), func=AF.Copy)) if eb % 2 else
                         (lambda e, pb=pb, eb=eb: e.tensor_copy(out=PP.ap([[1, 512]], off=eb * 512), in_=pb.ap([[1, 512]]))),
                         r=[pb.tok], w=[PP.tok])
                for kb in range(8):
                    rbuf = r3b[rot("r3b", 2)]
                    dma_in(rbuf, rbuf.ap([[1, 1024]]), dap(r3_d, kb * 1024, [[8192, 128], [1, 1024]]))
                    pb = PSB[nextps()]
                    for i in range(8):
                        k1 = kb * 8 + i
                        for c in range(2):
                            S.op("pe", lambda e, i=i, k1=k1, c=c, pb=pb, rbuf=rbuf, hl=hl: e.matmul(pb.ap([[1, 64]], off=i * 64, p0=hl * 64, pn=64), lhsT=PP.ap([[128, 64]], off=c * 64 + k1),
                                                                                             rhs=rbuf.ap([[1, 64]], off=(i * 2 + c) * 64), start=(c == 0), stop=(c == 1), tile_position=(0, hl * 64)),
                                 r=[PP.tok, rbuf.tok], w=[pb.tok])
                    S.op("dve", lambda e, pb=pb, kb=kb, hl=hl: e.tensor_copy(out=zs.ap([[1, 8], [64, 64]], off=kb * 8, p0=hl * 64, pn=64), in_=pb.ap([[64, 8], [1, 64]], p0=hl * 64, pn=64)),
                         r=[pb.tok], w=[zs.tok])
                if hl == 1:
                    S.op("pool", lambda e, p=p: e.dma_start(out=dap(zz_d, p * 128 * 4096, [[4096, 128], [1, 4096]]), in_=zs.ap([[1, 4096]])),
                         r=[zs.tok], w=[("zz_d", p)], dma="S_" + zs.tok)

            S.phase = 3
            S.op("dve", lambda e: e.memset(arA.ap([[1, 1]]), 0.0), w=[arA.tok])
            S.op("dve", lambda e: e.memset(arB.ap([[1, 1]]), 0.0), w=[arB.tok])
            yT = Buf(arA.t, 8192, "yT", 0, arA.tok)
            vw = Buf(arA.t, 8192, "vw", 4096, arA.tok)
            exb = [Buf(arA.t, 8192, "exb%d" % i, 4096 + 2080 + i * 320, arA.tok) for i in range(3)]
            ptb = [Buf(arA.t, 8192, "ptb%d" % i, 4096 + 2080 + 960 + i * 320, arA.tok) for i in range(3)]
            kw = Buf(arB.t, 8192, "kw", 0, arB.tok)
            gt = Buf(arB.t, 8192, "gt", 2048, arB.tok)
            ptm = [Buf(arB.t, 8192, "ptm%d" % i, 6144 + i * 1024, arB.tok) for i in range(2)]
            ug = [Buf(hs[i].t.bitcast(F32), 512, hs[i].tok) for i in range(2)]
            ft = sqb
            qz = [Buf(rb[i].t.bitcast(BF16), 1024, rb[i].tok) for i in range(2)]
            for i in range(2):
                S.op("dve", lambda e, i=i: e.memset(rb[i].ap([[1, 512]]), 0.0), w=[rb[i].tok])
            obb = vst[0]
            odb = vst[1]
            rec = st1[0]
            mv = st1[1]

            def win(g):
                tlo = min(max(8 * g - 4, 0), 56) // 2
                thi = (min(max(8 * g + 3, 0), 56) + 7) // 2
                return tlo, thi - tlo + 1

            def loads_a(g):
                hT = hTg[g % 2]
                ns = nst[g % 2]
                OP("sp", lambda e: e.dma_start(out=hT.ap([[1, 4096]]), in_=dap(hT_d, g * 128 * 4096, [[4096, 128], [1, 4096]])),
                   wr=[hT], er=[("hT_d", g)], dma="L_" + hT.tok)
                OP("sp", lambda e: e.dma_start(out=ns.ap([[512, 2], [1, 512]], off=2048), in_=dap(nrm_d, 4 * 128 * 4096 + g * 512, [[4096, 128], [128 * 4096, 2], [1, 512]])),
                   wr=[ns], er=[("nrm_d", g)], dma="L_" + ns.tok)
                OP("sp", lambda e: e.dma_start(out=ns.ap([[512, 2], [1, 512]]), in_=dap(zz_d, g * 512, [[4096, 128], [128 * 4096, 2], [1, 512]])),
                   wr=[ns], er=[("zz_d", 0), ("zz_d", 1)], dma="L_" + ns.tok)

            def loads_b(g):
                tlo, nt_ = win(g)
                OP("sp", lambda e: e.dma_start(out=kw.ap([[1024, 2], [1, nt_ * 128]]), in_=dap(nrm_d, tlo * 128, [[4096, 128], [128 * 4096, 2], [1, nt_ * 128]])),
                   wr=[kw], er=[("nrm_d", g_) for g_ in range(NG)], dma="L_kw")
                OP("sp", lambda e: e.dma_start(out=vw.ap([[1, nt_ * 260]]), in_=dap(va_d, tlo * 260, [[NT * 260, 128], [1, nt_ * 260]])),
                   wr=[vw], er=[("va_d", g_) for g_ in range(NG)], dma="L_vw")
                for p in range(2):
                    for hl in range(2):
                        OP("sp", lambda e, p=p, hl=hl: e.dma_start(out=qz[p].ap([[1, 512]], off=hl * 512, p0=hl * 64, pn=64), in_=dap(nrm_d, ((2 + p) * 128 + hl * 64) * 4096 + g * 512, [[4096, 64], [1, 512]])),
                           wr=[qz[p]], er=[("nrm_d", g)], dma="L_" + qz[p].tok)

            def proj3(hT, col):
                pb = PSB[nextps()]
                for kc in range(8):
                    OP("pe", lambda e, kc=kc: e.matmul(pb.ap([[1, 512]]), lhsT=Wb.ap([[1, 128]], off=kc * INC + col), rhs=hT.ap([[1, 512]], off=kc * 512),
                                                        start=(kc == 0), stop=(kc == 7)), rd=[WbB, hT], wr=[pb])
                return pb

            def do_group(g):
                hT = hTg[g % 2]
                ns = nst[g % 2]
                tlo, nt_ = win(g)
                if g + 1 < NG:
                    loads_a(g + 1)
                xbs = []
                for j in range(4):
                    t = g * 4 + j
                    xb = xs[rot("xs", 3)]
                    xbs.append(xb)
                    xsrc_ap = dap(xsrc, t * 128 * D, [[D, 128], [1, D]])
                    if j < 3:
                        OP("sp", lambda e, xb=xb, xsrc_ap=xsrc_ap: e.dma_start(out=xb.ap([[1, D]]), in_=xsrc_ap), wr=[xb], er=[("xo", t)] if l > 0 else [], dma="L_" + xb.tok)
                chunks = []

                def gate_chunk(si, col, p):
                    def f():
                        pb = proj3(hT, col + p * 128)
                        th = ft[rot("ft", 2)]
                        OP("act", lambda e: e.activation(out=th.ap([[1, 512]]), in_=pb.ap([[1, 512]]), func=AF.Tanh, scale=0.5), rd=[pb], wr=[th])
                        OP("dve", lambda e: e.scalar_tensor_tensor(out=gt.ap([[1, 512]], off=(si * 2 + p) * 512), in0=th.ap([[1, 512]]), scalar=1.0, in1=pb.ap([[1, 512]]), op0=ALU.add, op1=ALU.mult),
                           rd=[pb, th], wr=[gt])
                    return f

                for si, col in enumerate((C_AG, C_BG, C_CG, C_DG)):
                    for p in range(2):
                        chunks.append(gate_chunk(si, col, p))

                def c_chunk():
                    for p in range(2):
                        OP("dve", lambda e, p=p: e.tensor_tensor(out=yT.ap([[1, 512]], off=(4 + p) * 512), in0=ns.ap([[1, 512]], off=p * 512), in1=gt.ap([[1, 512]], off=(4 + p) * 512), op=ALU.mult),
                           rd=[ns, gt], wr=[yT])
                chunks.append(c_chunk)

                def u_chunk(p):
                    def f():
                        pb = proj3(hT, C_AU + p * 128)
                        OP("dve", lambda e: e.tensor_tensor(out=ug[p].ap([[1, 512]]), in0=pb.ap([[1, 512]]), in1=gt.ap([[1, 512]], off=p * 512), op=ALU.mult), rd=[pb, gt], wr=[ug[p]])
                    return f
                chunks.append(u_chunk(0))
                chunks.append(u_chunk(1))
                spbh = {}

                def v_chunk(j):
                    def f():
                        if j == 0:
                            spbh[0], spbh[1] = PSB[5], PSB[6]
                        spb = spbh
                        pb = PSB[nextps()]
                        for kc in range(8):
                            OP("pe", lambda e, kc=kc: e.matmul(pb.ap([[1, 256]]), lhsT=hT.ap([[1, 128]], off=kc * 512 + j * 128), rhs=Wb.ap([[1, 256]], off=kc * INC + C_AV),
                                                               start=(kc == 0), stop=(kc == 7)), rd=[hT, WbB], wr=[pb])
                        bst = ft[rot("ft", 2)]
                        OP("dve", lambda e: e.bn_stats(out=bst.ap([[1, 6]]), in_=pb.ap([[1, 256]])), rd=[pb], wr=[bst])
                        OP("dve", lambda e: e.bn_aggr(out=mv.ap([[1, 2]]), in_=bst.ap([[1, 6]])), rd=[bst], wr=[mv])
                        a = mv.ap([[1, 1]], off=1)
                        OP("act", lambda e: e.activation(out=a, in_=a, func=AF.Sqrt, bias=EPS), rd=[mv], wr=[mv])
                        OP("dve", lambda e: e.reciprocal(out=a, in_=a), rd=[mv], wr=[mv])
                        tmp = ft[rot("ft", 2)]
                        OP("dve", lambda e: e.tensor_scalar(out=tmp.ap([[1, 256]]), in0=pb.ap([[1, 256]]), scalar1=mv.ap([[1, 1]]), scalar2=mv.ap([[1, 1]], off=1), op0=ALU.subtract, op1=ALU.mult),
                           rd=[pb, mv], wr=[tmp])
                        OP("dve", lambda e: e.tensor_tensor(out=tmp.ap([[1, 256]]), in0=tmp.ap([[1, 256]]), in1=par.ap([[1, 256]], off=20), op=ALU.mult), rd=[tmp, parL], wr=[tmp])
                        vn = cst[rot("cst", 2)]
                        OP("dve", lambda e: e.tensor_tensor(out=vn.ap([[1, 256]]), in0=tmp.ap([[1, 256]]), in1=par.ap([[1, 256]], off=276), op=ALU.add), rd=[tmp, parL], wr=[vn])
                        for h in range(4):
                            p, hl = h // 2, h % 2
                            OP("pe", lambda e, h=h, p=p, hl=hl: e.matmul(spb[p].ap([[1, 128]], off=j * 128, p0=hl * 64, pn=64), lhsT=vn.ap([[1, 64]], off=h * 64), rhs=wsb.ap([[1, 128]], off=h * 128),
                                                                     start=True, stop=True, tile_position=(0, hl * 64)), rd=[vn, wsb], wr=[spb[p]])
                        if j == 3:
                            for p in range(2):
                                t1 = ft[rot("ft", 2)]
                                OP("dve", lambda e, p=p, t1=t1: e.tensor_tensor(out=t1.ap([[128, 4], [1, 128]]), in0=spb[p].ap([[128, 4], [1, 128]]), in1=par.ap([[0, 4], [1, 128]], off=532 + p * 128), op=ALU.add),
                                   rd=[spb[p], parL], wr=[t1])
                                OP("dve", lambda e, p=p, t1=t1: e.tensor_tensor(out=yT.ap([[1, 512]], off=p * 512), in0=t1.ap([[1, 512]]), in1=ug[p].ap([[1, 512]]), op=ALU.mult), rd=[t1, ug[p]], wr=[yT])
                    return f
                for j in range(4):
                    chunks.append(v_chunk(j))

                def d_chunk(h):
                    def f():
                        p, hl = h // 2, h % 2
                        pm = ptm[h % 2]
                        for mt in range(2):
                            pb = PSB[nextps()]
                            OP("pe", lambda e, pb=pb, mt=mt: e.matmul(pb.ap([[1, 512]]), lhsT=kmnz.ap([[1, 128]], off=(p * 2 + hl) * 256 + mt * 128),
                                                                      rhs=ns.ap([[1, 512]], off=(4 + p) * 512), start=True, stop=True), rd=[kmnz, ns], wr=[pb])
                            OP("act", lambda e, pb=pb, mt=mt: e.activation(out=pm.ap([[1, 512]], off=mt * 512), in_=pb.ap([[1, 512]]), func=AF.Exp), rd=[pb], wr=[pm])
                        ob_ = PSB[5 + h % 2]
                        for j in range(4):
                            for mt in range(2):
                                OP("pe", lambda e, j=j, mt=mt: e.matmul(ob_.ap([[1, 65]], off=j * 65), lhsT=pm.ap([[1, 128]], off=mt * 512 + j * 128), rhs=Vm.ap([[1, 65]], off=mt * 260 + h * 65),
                                                                        start=(mt == 0), stop=(mt == 1)), rd=[pm, Vm], wr=[ob_])
                        rc = Buf(rec.t, rec.rl, "recD", base=4)
                        OP("dve", lambda e: e.reciprocal(out=rc.ap([[1, 4]]), in_=ob_.ap([[65, 4]], off=64)), rd=[ob_], wr=[rc], er=[rec.tok])
                        OP("dve", lambda e: e.tensor_tensor(out=odb.ap([[256, 4], [1, 64]], off=h * 64), in0=ob_.ap([[65, 4], [1, 64]]), in1=rc.ap([[1, 4], [0, 64]]), op=ALU.mult),
                           rd=[ob_, rc], wr=[odb], er=[rec.tok])
                    return f
                for h in range(4):
                    chunks.append(d_chunk(h))

                def dT_chunk(p):
                    def f():
                        tb = psb16[5 + p]
                        for j in range(4):
                            OP("pe", lambda e, j=j: e.transpose(out=tb.ap([[1, 128]], off=j * 128), in_=odb.ap([[1, 128]], off=j * 256 + p * 128), identity=identb.ap([[1, 128]])),
                               rd=[odb, identb], wr=[tb])
                        OP("dve", lambda e: e.tensor_tensor(out=yT.ap([[1, 512]], off=(6 + p) * 512), in0=tb.ap([[1, 512]]), in1=gt.ap([[1, 512]], off=(6 + p) * 512), op=ALU.mult),
                           rd=[tb, gt], wr=[yT])
                    return f
                chunks.append(dT_chunk(0))
                chunks.append(dT_chunk(1))

                ob_B = PSB[7]
                rcB = Buf(rec.t, rec.rl, "recB", base=0)
                for j in range(4):
                    for rql in range(2):
                        rq = g * 8 + j * 2 + rql
                        rs_ = min(max(rq - 4, 0), 56)
                        tiles = list(range(rs_ // 2, (rs_ + 7) // 2 + 1))
                        nk = len(tiles)
                        m0 = 2 * tiles[0] - rq + 8
                        for h in range(4):
                            p, hl = h // 2, h % 2
                            sbk = PSB[nextps()]
                            for ki, i in enumerate(tiles):
                                OP("pe", lambda e, ki=ki, i=i, p=p, hl=hl, sbk=sbk, rq=rq: e.matmul(sbk.ap([[1, 64]], off=ki * 64), lhsT=kw.ap([[1, 128]], off=p * 1024 + (i - tlo) * 128),
                                                                                               rhs=qz[p].ap([[1, 64]], off=hl * 512 + (rq % 8) * 64), start=True, stop=True),
                                   rd=[kw, qz[p]], wr=[sbk])
                            ex = exb[rot("exb", 3)]
                            pt = ptb[rot("ptb", 3)]
                            OP("act", lambda e, sbk=sbk, ex=ex, nk=nk: e.activation(out=ex.ap([[1, nk * 64]]), in_=sbk.ap([[1, nk * 64]]), func=AF.Exp), rd=[sbk], wr=[ex])
                            OP("dve", lambda e, ex=ex, pt=pt, nk=nk, h=h, m0=m0: e.tensor_tensor(out=pt.ap([[64, nk], [1, 64]]), in0=ex.ap([[64, nk], [1, 64]]), in1=EB.ap([[128, nk], [1, 64]], off=h * 1024 + m0 * 64), op=ALU.mult),
                               rd=[ex, EB], wr=[pt])
                            if chunks:
                                chunks.pop(0)()
                            for ki, i in enumerate(tiles):
                                lo_ok = (2 * i >= rs_)
                                hi_ok = (2 * i + 1 <= rs_ + 7)
                                if not (lo_ok and hi_ok):
                                    z0 = 0 if hi_ok else 64
                                    OP("dve", lambda e, ki=ki, z0=z0, pt=pt: e.memset(pt.ap([[1, 64]], off=ki * 64, p0=z0, pn=64), 0.0), rd=[pt], wr=[pt])
                            for ki, i in enumerate(tiles):
                                OP("pe", lambda e, ki=ki, i=i, pt=pt, h=h, rql=rql, nk=nk: e.matmul(ob_B.ap([[1, 65]], off=h * 65, p0=rql * 64, pn=64), lhsT=pt.ap([[1, 64]], off=ki * 64),
                                                                                                rhs=vw.ap([[1, 65]], off=(i - tlo) * 260 + h * 65), start=(ki == 0), stop=(ki == nk - 1), tile_position=(0, rql * 64)),
                                   rd=[pt, vw], wr=[ob_B])
                    OP("dve", lambda e: e.reciprocal(out=rcB.ap([[1, 4]]), in_=ob_B.ap([[65, 4]], off=64)), rd=[ob_B], wr=[rcB], er=[rec.tok])
                    OP("dve", lambda e, j=j: e.tensor_tensor(out=obb.ap([[64, 4], [1, 64]], off=j * 256), in0=ob_B.ap([[65, 4], [1, 64]]), in1=rcB.ap([[1, 4], [0, 64]]), op=ALU.mult),
                       rd=[ob_B, rcB], wr=[obb], er=[rec.tok])
                while chunks:
                    chunks.pop(0)()
                for p in range(2):
                    tb = psb16[5 + p]
                    for j in range(4):
                        OP("pe", lambda e, j=j, p=p, tb=tb: e.transpose(out=tb.ap([[1, 128]], off=j * 128), in_=obb.ap([[1, 128]], off=j * 256 + p * 128), identity=identb.ap([[1, 128]])),
                           rd=[obb, identb], wr=[tb])
                    OP("dve", lambda e, p=p, tb=tb: e.tensor_tensor(out=yT.ap([[1, 512]], off=(2 + p) * 512), in0=tb.ap([[1, 512]]), in1=gt.ap([[1, 512]], off=(2 + p) * 512), op=ALU.mult),
                       rd=[tb, gt], wr=[yT])
                if g + 1 < NG:
                    loads_b(g + 1)
                for j in range(4):
                    t = g * 4 + j
                    xb = xbs[j]
                    if j == 3:
                        xsrc_ap = dap(xsrc, t * 128 * D, [[D, 128], [1, D]])
                        OP("sp", lambda e, xb=xb, xsrc_ap=xsrc_ap: e.dma_start(out=xb.ap([[1, D]]), in_=xsrc_ap), wr=[xb], er=[("xo", t)] if l > 0 else [], dma="L_" + xb.tok)
                    for n in range(2):
                        pb = PSB[nextps()]
                        for kc in range(8):
                            OP("pe", lambda e, kc=kc, j=j, n=n, pb=pb: e.matmul(pb.ap([[1, 512]]), lhsT=yT.ap([[1, 128]], off=kc * 512 + j * 128), rhs=Wob.ap([[1, 512]], off=kc * D + n * 512),
                                                                            start=(kc == 0), stop=(kc == 7)), rd=[yT, Wob], wr=[pb])
                        OP("dve", lambda e, n=n, pb=pb, xb=xb: e.tensor_tensor(out=xb.ap([[1, 512]], off=n * 512), in0=pb.ap([[1, 512]]), in1=xb.ap([[1, 512]], off=n * 512), op=ALU.add), rd=[pb, xb], wr=[xb])
                    OP("pool", lambda e, xb=xb, t=t: e.dma_start(out=dap(out_d, t * 128 * D, [[D, 128], [1, D]]), in_=xb.ap([[1, D]])), rd=[xb], ew=[("xo", t)], dma="S_" + xb.tok)

            loads_a(0)
            loads_b(0)
            for g in range(NG):
                do_group(g)

        S.op("sp", None, r=[("xo", t) for t in range(NT)])
        S.emit(nc, es)
    return nc


def _consts():
    bf = ml_dtypes.bfloat16
    identb = np.eye(128, dtype=np.float32).astype(bf)
    bo = np.zeros((128, 128), np.float32)
    bo[:64, :64] = 1.0 / 64
    bo[64:, 64:] = 1.0 / 64
    bo = bo.astype(bf)
    e = np.arange(64)
    ang = 2 * np.pi * np.outer(e, e) / 64.0
    c64 = np.concatenate([np.concatenate([np.cos(ang) / 512.0, -np.sin(ang) / 512.0], axis=1), np.zeros((64, 128))], axis=0).astype(np.float32).astype(bf)
    r1 = np.concatenate([np.concatenate([np.cos(ang), np.sin(ang)], axis=1), np.zeros((64, 128))], axis=0).astype(np.float32).astype(bf)
    s2 = np.arange(64)[:, None, None]
    k1 = np.arange(64)[None, :, None]
    k2 = np.arange(64)[None, None, :]
    th = 2 * np.pi * (((k1 + 64 * k2) * s2) % 4096) / 4096.0
    ct, st = np.cos(th), np.sin(th)
    r3 = np.zeros((2, 64, 64, 2, 64), np.float64)
    r3[0, :, :, 0, :] = ct
    r3[1, :, :, 0, :] = st
    r3[0, :, :, 1, :] = -st
    r3[1, :, :, 1, :] = ct
    r3 = r3.reshape(128, 8192).astype(np.float32).astype(bf)
    return identb, bo, c64, r1, r3


def _layer_params(inp):
    f = np.float32
    par = np.zeros((2, 128, NPAR), f)
    wst = np.zeros((2, 128, 512), f)
    fnw = np.zeros((2, 64, 256), f)
    biasT = np.zeros((2, 128, 4096), f)
    ck = np.arange(64)[:, None]
    cq = np.arange(64)[None, :]
    cs = np.clip(cq - 8, 0, 48)
    colok = (ck >= cs) & (ck < cs + 16)
    dcol = np.clip(ck - cq + 15, 0, 30)
    for l in range(2):
        par[l, :, 0:8] = inp["norm_g"][l].reshape(8, 128).T
        par[l, :, 8:16] = inp["mem_norm_g"][l].reshape(8, 128).T
        par[l, :, 16] = np.tile(inp["na_qn_g"][l], 2)
        par[l, :, 17] = np.tile(inp["na_kn_g"][l], 2)
        par[l, :, 18] = np.tile(inp["mem_qn_g"][l], 2)
        par[l, :, 19] = np.tile(inp["mem_kn_g"][l], 2)
        par[l, :, 20:276] = np.broadcast_to(inp["gm_ln_g"][l][None, :], (128, 256))
        par[l, :, 276:532] = np.broadcast_to(inp["gm_ln_b"][l][None, :], (128, 256))
        bs = inp["gm_b_s"][l]
        par[l, :, 532:788] = np.repeat(bs.reshape(2, 2, 1, 128), 64, axis=2).transpose(1, 2, 0, 3).reshape(128, 256)
        wst[l] = inp["gm_w_s"][l].transpose(2, 0, 1).reshape(128, 512)
        fnw[l] = inp["fn_w"][l].transpose(1, 0, 2).reshape(64, 256)
        rpb = inp["na_rpb"][l]
        bt = np.full((2, 64, 4, 16, 64), -30000.0, f)
        for rkl in range(2):
            for m in range(16):
                dr = m - 8 + rkl
                if -7 <= dr <= 7:
                    vals = rpb[:, dr + 7, :][:, dcol]
                    bt[rkl, :, :, m, :] = np.where(colok[None], vals, f(-30000.0)).transpose(1, 0, 2)
        biasT[l] = bt.reshape(128, 4096)
    return par, wst, fnw, biasT


_CACHE = {}


def _in_maps(inp):
    identb, bo, c64, r1, r3 = _consts()
    par, wst, fnw, biasT = _layer_params(inp)
    common = {
        "w_in": np.ascontiguousarray(inp["w_in"], np.float32), "w_out": np.ascontiguousarray(inp["w_out"], np.float32),
        "w_kv": np.ascontiguousarray(inp["mem_w_kv"], np.float32), "par": par, "wst": wst, "fnw": fnw, "biasT": biasT,
        "identb": identb, "bo": bo, "c64": c64, "r1": r1, "r3": r3,
    }
    maps = []
    for b in range(8):
        m = dict(common)
        m["x"] = np.ascontiguousarray(inp["x"][b], np.float32)
        m["mem"] = np.ascontiguousarray(inp["mem"][b], np.float32)
        maps.append(m)
    return maps


def kernel(**inputs):
    inp = {k: np.asarray(v) for k, v in inputs.items()}
    if "nc" not in _CACHE:
        _CACHE["nc"] = build()
    res = run_bass_kernel_spmd(_CACHE["nc"], _in_maps(inp), core_ids=list(range(8)))
    return np.stack([np.asarray(r["out"], np.float32) for r in res.results], axis=0)
```
